# Optimizing a Trainium2 kernel written in Bass

```python
import jax, jax.numpy as jnp
from jax import lax
import numpy as np

D_MODEL = 1024
BATCH = 4
SEQ = 8192
DEPTH = 4

CTX_LEN = 256
GRID_W = 64
FOURIER_WIDTH = 256
FOURIER_GROUPS = 4
SGU_WIDTH = 256
SGU_GROUPS = 4
CHUNK = 128
N_HEADS = 8
N_KV_HEADS = 2
HEAD_DIM = 64
ATTN_WIDTH = N_HEADS * HEAD_DIM
KV_WIDTH = N_KV_HEADS * HEAD_DIM
WINDOW = 128
BLOCK = 128
ROPE_THETA = 10000.0
N_BRANCHES = 3
EPS = 1e-6
NEG_INF = -1e30
IN_WIDTH = 2 * FOURIER_WIDTH + 3 * SGU_WIDTH + 2 * ATTN_WIDTH + 2 * KV_WIDTH + N_BRANCHES * D_MODEL
KV_OFFSET = 2 * FOURIER_WIDTH + 3 * SGU_WIDTH + ATTN_WIDTH

kernel_name = "hybrid_fourier_sgu_window_gqa_trunk"


def _split_proj(p):
    sizes = (FOURIER_WIDTH, FOURIER_WIDTH, SGU_WIDTH, SGU_WIDTH, SGU_WIDTH,
             ATTN_WIDTH, KV_WIDTH, KV_WIDTH, ATTN_WIDTH, N_BRANCHES * D_MODEL)
    offs = [int(o) for o in np.cumsum(sizes)[:-1]]
    return jnp.split(p, offs, axis=-1)


def _rms(x):
    x32 = x.astype(jnp.float32)
    return (x32 * lax.rsqrt(jnp.mean(x32 * x32, axis=-1, keepdims=True) + EPS)).astype(x.dtype)


def _axial_rope(x, row, col):
    half = HEAD_DIM // 2
    nf = half // 2
    inv = ROPE_THETA ** (-jnp.arange(nf, dtype=jnp.float32) / nf)

    def rot(xp, pos):
        ang = pos.astype(jnp.float32)[:, None] * inv[None, :]
        cos = jnp.cos(ang)[None, :, None, :]
        sin = jnp.sin(ang)[None, :, None, :]
        x1, x2 = xp[..., :nf], xp[..., nf:]
        return jnp.concatenate([x1 * cos - x2 * sin, x2 * cos + x1 * sin], axis=-1)

    x32 = x.astype(jnp.float32)
    return jnp.concatenate([rot(x32[..., :half], row), rot(x32[..., half:], col)], axis=-1).astype(x.dtype)


def _fourier(f):
    B, L, W = f.shape
    fg = f.reshape(B, L, FOURIER_GROUPS, W // FOURIER_GROUPS).astype(jnp.float32)
    y = jnp.fft.fft2(fg, axes=(1, 3), norm="ortho").real
    return y.reshape(B, L, W).astype(f.dtype)


def _sgu(u, v, w_s, b_s):
    B, L, W = v.shape
    nc = L // CHUNK
    gw = W // SGU_GROUPS
    vn = _rms(v).reshape(B, nc, CHUNK, SGU_GROUPS, gw)
    mixed = jnp.einsum('gst,bctgd->bcsgd', w_s, vn) + b_s.T[:, :, None]
    return u * mixed.reshape(B, L, W)


def _latent_attn(q, k, v, kc, vc, sinks):
    B, L = q.shape[0], q.shape[1]
    C = kc.shape[1]
    nb = L // BLOCK
    G = N_HEADS // N_KV_HEADS
    qb = q.reshape(B, nb, BLOCK, N_KV_HEADS, G, HEAD_DIM)
    pad = ((0, 0), (BLOCK, BLOCK), (0, 0), (0, 0))
    kp = jnp.pad(k, pad).reshape(B, nb + 2, BLOCK, N_KV_HEADS, HEAD_DIM)
    vp = jnp.pad(v, pad).reshape(B, nb + 2, BLOCK, N_KV_HEADS, HEAD_DIM)

    def band(t):
        return jnp.concatenate([t[:, :-2], t[:, 1:-1], t[:, 2:]], axis=2)

    kb, vb = band(kp), band(vp)
    qi = jnp.arange(BLOCK)[:, None]
    kj = jnp.arange(3 * BLOCK)[None, :]
    in_win = jnp.abs(kj - BLOCK - qi) <= WINDOW
    kpos = jnp.arange(nb)[:, None] * BLOCK - BLOCK + jnp.arange(3 * BLOCK)[None, :]
    in_seq = (kpos >= 0) & (kpos < L)
    mask = in_win[None] & in_seq[:, None, :]
    scale = HEAD_DIM ** -0.5
    s_loc = jnp.einsum('bnqhgd,bnkhd->bnhgqk', qb, kb).astype(jnp.float32) * scale
    s_loc = jnp.where(mask[None, :, None, None], s_loc, NEG_INF)
    s_ctx = jnp.einsum('bnqhgd,bchd->bnhgqc', qb, kc).astype(jnp.float32) * scale
    s_sink = jnp.broadcast_to(sinks.astype(jnp.float32).reshape(N_KV_HEADS, G, 1, 1), s_loc.shape[:-1] + (1,))
    probs = jax.nn.softmax(jnp.concatenate([s_loc, s_ctx, s_sink], axis=-1), axis=-1)
    p_loc = probs[..., :3 * BLOCK].astype(v.dtype)
    p_ctx = probs[..., 3 * BLOCK:3 * BLOCK + C].astype(v.dtype)
    o = (jnp.einsum('bnhgqk,bnkhd->bnqhgd', p_loc, vb)
         + jnp.einsum('bnhgqc,bchd->bnqhgd', p_ctx, vc))
    return o.reshape(B, L, ATTN_WIDTH)


def _ctx_attn(q, k, v, sinks):
    B, C = q.shape[0], q.shape[1]
    G = N_HEADS // N_KV_HEADS
    qg = q.reshape(B, C, N_KV_HEADS, G, HEAD_DIM)
    s = jnp.einsum('bqhgd,bkhd->bhgqk', qg, k).astype(jnp.float32) * (HEAD_DIM ** -0.5)
    s_sink = jnp.broadcast_to(sinks.astype(jnp.float32).reshape(N_KV_HEADS, G, 1, 1), s.shape[:-1] + (1,))
    probs = jax.nn.softmax(jnp.concatenate([s, s_sink], axis=-1), axis=-1)[..., :C].astype(v.dtype)
    return jnp.einsum('bhgqk,bkhd->bqhgd', probs, v).reshape(B, C, ATTN_WIDTH)


def _mixer_out(fa, za, u, vs, zb, attn, zc, g, w_s, b_s, w_pa, w_pb, w_pc, w_out):
    ya = _fourier(fa) * jax.nn.silu(za)
    yb = _sgu(u, vs, w_s, b_s) * jax.nn.silu(zb)
    yc = attn * jax.nn.silu(zc)
    ga, gb, gc = jnp.split(jax.nn.sigmoid(g), N_BRANCHES, axis=-1)
    merged = ga * (ya @ w_pa) + gb * (yb @ w_pb) + gc * (yc @ w_pc)
    return merged @ w_out


def _qk(t, n_heads, gain):
    B, L = t.shape[0], t.shape[1]
    return _rms(t.reshape(B, L, n_heads, HEAD_DIM)) * gain


def setup_inputs(seed: int = 0) -> dict:
    key = jax.random.key(seed)
    ks = jax.random.split(key, 16)
    f32 = jnp.float32
    D = D_MODEL
    return {
        "x": jax.random.normal(ks[0], (BATCH, SEQ, D), f32),
        "c": jax.random.normal(ks[1], (BATCH, D), f32),
        "ctx": jax.random.normal(ks[2], (BATCH, CTX_LEN, D), f32),
        "c_ctx": jax.random.normal(ks[3], (D,), f32),
        "w_ada": jax.random.normal(ks[4], (DEPTH, D, 3 * D), f32) * (0.5 * D ** -0.5),
        "b_ada": jax.random.normal(ks[5], (DEPTH, 3 * D), f32) * 0.02,
        "w_in": jax.random.normal(ks[6], (DEPTH, D, IN_WIDTH), f32) * D ** -0.5,
        "sgu_w": jax.random.normal(ks[7], (DEPTH, SGU_GROUPS, CHUNK, CHUNK), f32) * CHUNK ** -0.5,
        "sgu_b": 1.0 + 0.02 * jax.random.normal(ks[8], (DEPTH, SGU_GROUPS, CHUNK), f32),
        "q_norm_g": 1.0 + 0.02 * jax.random.normal(ks[9], (DEPTH, HEAD_DIM), f32),
        "k_norm_g": 1.0 + 0.02 * jax.random.normal(ks[10], (DEPTH, HEAD_DIM), f32),
        "attn_sink": jax.random.normal(ks[11], (DEPTH, N_HEADS), f32),
        "w_pa": jax.random.normal(ks[12], (DEPTH, FOURIER_WIDTH, D), f32) * FOURIER_WIDTH ** -0.5,
        "w_pb": jax.random.normal(ks[13], (DEPTH, SGU_WIDTH, D), f32) * SGU_WIDTH ** -0.5,
        "w_pc": jax.random.normal(ks[14], (DEPTH, ATTN_WIDTH, D), f32) * ATTN_WIDTH ** -0.5,
        "w_out": jax.random.normal(ks[15], (DEPTH, D, D), f32) * D ** -0.5,
    }


def reference(x, c, ctx, c_ctx, w_ada, b_ada, w_in, sgu_w, sgu_b, q_norm_g, k_norm_g,
              attn_sink, w_pa, w_pb, w_pc, w_out):
    B, L, _ = x.shape
    C = ctx.shape[1]
    rows = L // GRID_W
    row = jnp.repeat(jnp.arange(rows), GRID_W)
    col = jnp.tile(jnp.arange(GRID_W), rows)
    s_c = jax.nn.silu(c)
    s_cc = jax.nn.silu(c_ctx)
    xc = ctx
    for l in range(DEPTH):
        last = l == DEPTH - 1
        shift, scale, gate = jnp.split(s_c @ w_ada[l] + b_ada[l], 3, axis=-1)
        shift_c, scale_c, gate_c = jnp.split(s_cc @ w_ada[l] + b_ada[l], 3, axis=-1)
        h = _rms(x) * (1.0 + scale[:, None, :]) + shift[:, None, :]
        hc = _rms(xc) * (1.0 + scale_c) + shift_c

        fa, za, u, vs, zb, q, k, va, zc, g = _split_proj(h @ w_in[l])
        q = _axial_rope(_qk(q, N_HEADS, q_norm_g[l]), row, col)
        k = _axial_rope(_qk(k, N_KV_HEADS, k_norm_g[l]), row, col)
        va = va.reshape(B, L, N_KV_HEADS, HEAD_DIM)

        if last:
            kc, vc = jnp.split(hc @ w_in[l][:, KV_OFFSET:KV_OFFSET + 2 * KV_WIDTH], 2, axis=-1)
        else:
            fa_c, za_c, u_c, vs_c, zb_c, qc, kc, vc, zc_c, g_c = _split_proj(hc @ w_in[l])
        kc = _qk(kc, N_KV_HEADS, k_norm_g[l])
        vc = vc.reshape(B, C, N_KV_HEADS, HEAD_DIM)

        attn = _latent_attn(q, k, va, kc, vc, attn_sink[l])
        out = _mixer_out(fa, za, u, vs, zb, attn, zc, g, sgu_w[l], sgu_b[l],
                         w_pa[l], w_pb[l], w_pc[l], w_out[l])

        if not last:
            qc = _qk(qc, N_HEADS, q_norm_g[l])
            attn_c = _ctx_attn(qc, kc, vc, attn_sink[l])
            out_c = _mixer_out(fa_c, za_c, u_c, vs_c, zb_c, attn_c, zc_c, g_c, sgu_w[l], sgu_b[l],
                               w_pa[l], w_pb[l], w_pc[l], w_out[l])
            xc = xc + gate_c * out_c
        x = x + gate[:, None, :] * out
    return x
```

```python
import contextlib
import numpy as np
import ml_dtypes
import concourse.bass as bass
import concourse.mybir as mybir
from concourse.bass_utils import run_bass_kernel_spmd

F32 = mybir.dt.float32
BF16 = mybir.dt.bfloat16
AF = mybir.ActivationFunctionType
ALU = mybir.AluOpType
AX = mybir.AxisListType

DEBUG = False
STOP = None
D = 1024
CT = 2
EPS = 1e-6
NEG = -30000.0


class Sched:
    def __init__(self, nc, es):
        self.nc = nc
        self.es = es
        self.eng = {'pe': nc.tensor, 'act': nc.scalar, 'dve': nc.vector,
                    'pool': nc.gpsimd, 'sp': nc.sync}
        self.sem = {}
        self.cnt = {}
        for k in self.eng:
            self.sem[k] = es.enter_context(nc.semaphore("s_" + k))
            self.cnt[k] = 0
        self.lastw = {}
        self.readers = {}
        self.seen = {}
        self.nchan = 0
        self.chans = {}
        self.pe_kind = None

    def chan(self, name):
        if name in self.chans:
            return self.chans[name]
        k = "ch%d" % self.nchan
        self.nchan += 1
        self.sem[k] = self.es.enter_context(self.nc.semaphore("d_" + k))
        self.cnt[k] = 0
        self.chans[name] = k
        return k

    def _need(self, reads, writes):
        need = {}

        def add(x):
            if x is None:
                return
            if need.get(x[0], 0) < x[1]:
                need[x[0]] = x[1]
        for r in reads:
            add(self.lastw.get(r))
        for w in writes:
            add(self.lastw.get(w))
            for rd in self.readers.get(w, ()):
                add(rd)
        return need

    def _waits(self, e, need):
        for k, c in need.items():
            if k == e and e == 'pe':
                continue
            if self.seen.get((e, k), 0) >= c:
                continue
            self.eng[e].wait_ge(self.sem[k], c)
            self.seen[(e, k)] = c

    def _commit(self, key, reads, writes):
        tok = (key, self.cnt[key])
        for r in reads:
            self.readers.setdefault(r, []).append(tok)
        for w in writes:
            self.lastw[w] = tok
            self.readers[w] = []

    def op(self, e, fn, reads=(), writes=(), kind='M'):
        pb = [r for r in reads if len(r) == 2 and r[0] == 'b' and r[1].isdigit()]
        if pb:
            writes = list(writes) + [r for r in pb if r not in writes]
        self._waits(e, self._need(reads, writes))
        if e == 'pe':
            if self.pe_kind is not None and kind != self.pe_kind and self.cnt['pe'] > 0:
                if self.seen.get(('pe', 'pe'), 0) < self.cnt['pe']:
                    self.eng['pe'].wait_ge(self.sem['pe'], self.cnt['pe'])
                    self.seen[('pe', 'pe')] = self.cnt['pe']
            self.pe_kind = kind
        ins = fn(self.eng[e])
        self.cnt[e] += 1
        ins.then_inc(self.sem[e], 1)
        self._commit(e, reads, writes)

    def dma(self, e, chname, out, in_, reads=(), writes=(), **kw):
        ch = self.chan(chname)
        need = self._need(reads, writes)
        if self.cnt[ch] > 0:
            need[ch] = max(need.get(ch, 0), self.cnt[ch])
        self._waits(e, need)
        ins = self.eng[e].dma_start(out=out, in_=in_, **kw)
        self.cnt[ch] += 16
        ins.then_inc(self.sem[ch], 16)
        self._commit(ch, reads, writes)

    def barrier(self):
        for e in self.eng:
            need = {k: c for k, c in self.cnt.items() if c > 0 and k != e}
            self._waits(e, need)

    def finish(self, e='sp'):
        need = {k: c for k, c in self.cnt.items() if c > 0 and k != e}
        self._waits(e, need)


class Arena:
    def __init__(self, tensor, nelem):
        self.t = tensor
        self.n = nelem
        self.off = 0

    def seek(self, off):
        self.off = off

    def alloc(self, shape, dt):
        n = 1
        for s in shape[1:]:
            n *= s
        nb = n * (4 if dt == F32 else 2)
        nb = (nb + 255) // 256 * 256
        ne = nb // 2
        assert self.off + ne <= self.n, ("arena overflow", self.off, ne, self.n)
        ap = self.t[:, self.off:self.off + n * (2 if dt == F32 else 1)]
        self.off += ne
        if dt == F32:
            ap = ap.bitcast(F32)
        if len(shape) == 3:
            ap = ap.rearrange("p (a b) -> p a b", a=shape[1])
        elif len(shape) == 4:
            ap = ap.rearrange("p (a b c) -> p a b c", a=shape[1], b=shape[2])
        return ap


def build(NT, NL):
    L = NT * 128
    NTT = NT + CT
    K2 = 2 * NT
    nc = bass.Bass("TRN2", target_bir_lowering=False)
    es = contextlib.ExitStack()
    S = Sched(nc, es)

    def din(name, shape, dt=F32):
        return nc.dram_tensor(name, shape, dt, kind="ExternalInput").ap()

    def dscr(name, shape, dt):
        return nc.dram_tensor(name, shape, dt, kind="ExternalOutput" if DEBUG else "Internal").ap()

    x_in = din("x", [L, D])
    ctx_in = din("ctx", [CT * 128, D])
    ccol_in = din("ccol", [128, 8, 2])
    w_ada = din("w_ada", [NL, D, 3 * D])
    b_col = din("b_col", [NL, 128, 24])
    w_inA = din("w_inA", [NL, D, 2304])
    w_faT = din("w_faT", [NL, 256, D])
    w_g = din("w_g", [NL, D, 3 * D])
    w_pa = din("w_pa", [NL, 256, D])
    w_pb = din("w_pb", [NL, 256, D])
    w_pc = din("w_pc", [NL, 512, D])
    w_out = din("w_out", [NL, D, D])
    sgu_w_in = din("sgu_w", [NL, 4, 128, 128])
    sgu_bT = din("sgu_bT", [NL, 128, 4])
    qg_in = din("qg", [NL, 64])
    kg_in = din("kg", [NL, 64])
    sink_in = din("sink", [NL, 8])
    ident_in = din("ident", [128, 128], BF16)
    csblk_in = din("csblk", [256, 512], BF16)
    r1_in = din("r1", [K2, K2], BF16)
    m2_in = din("m2", [128, NT, 2, 128], BF16)
    c256_in = din("c256", [128, 2, 2, 256], BF16)
    ropeC_in = din("ropeC", [L, 64])
    ropeS_in = din("ropeS", [L, 64])
    maskP_in = din("maskP", [128, 512], BF16)
    maskN_in = din("maskN", [128, 512], BF16)
    y = nc.dram_tensor("y", [L, D], F32, kind="ExternalOutput").ap()

    xc_s = dscr("xc_s", [CT * 128, D], F32)
    gate_s = dscr("gate_s", [2, D], F32)
    faT_s = dscr("faT_s", [512, L], BF16)
    hT_s = dscr("hT_s", [NTT, 128, D], BF16)
    QT_s = dscr("QT_s", [NTT, 128, 512], BF16)
    KT_s = dscr("KT_s", [128, L], BF16)
    V_s = dscr("V_s", [L, 130], BF16)
    ybT_s = dscr("ybT_s", [NTT, 128, 256], BF16)
    sza_s = dscr("sza_s", [NTT * 128, 512], BF16)
    szc_s = dscr("szc_s", [NTT * 128, 512], BF16)
    Y_s = dscr("Y_s", [4, NTT * 128, 64], BF16)

    AR_ELEMS = 103 * 1024
    arena_t = es.enter_context(nc.sbuf_tensor("arena", [128, AR_ELEMS], BF16))
    A = Arena(arena_t, AR_ELEMS)
    ident = A.alloc([128, 128], BF16)
    csblk = A.alloc([128, 2, 512], BF16)
    r1 = A.alloc([128, K2], BF16)
    maskP = A.alloc([128, 512], BF16)
    maskN = A.alloc([128, 512], BF16)
    c256 = A.alloc([128, 2, 2, 256], BF16)
    gq = A.alloc([128, 64], F32)
    gk = A.alloc([128, 64], F32)
    expsink = A.alloc([128, 8], F32)
    wsT = A.alloc([128, 4, 128], BF16)
    sgub = A.alloc([128, 4], F32)
    scol = A.alloc([128, 8, 2], F32)
    modsb = A.alloc([128, 24, 2], F32)
    bcol = A.alloc([128, 24], F32)
    KcT = A.alloc([128, CT * 128], BF16)
    Vc = A.alloc([128, CT, 2, 65], BF16)
    fac = A.alloc([128, CT, 512], BF16)
    mhalf = A.alloc([128, 8], F32)
    Vst = [A.alloc([128, 2, 65], BF16) for _ in range(2)]
    P_END = A.off
    WA_OFF = P_END
    wA = A.alloc([128, 8, 2816], BF16)
    T_OFF = A.off
    T_SIZE = 34 * 1024
    WC_OFF = T_OFF + T_SIZE
    A.seek(WC_OFF)
    wg = A.alloc([128, 8, 3072], BF16)
    wpa = A.alloc([128, 2, 1024], BF16)
    wpb = A.alloc([128, 2, 1024], BF16)
    wpc = A.alloc([128, 4, 1024], BF16)
    wout = A.alloc([128, 8, 1024], BF16)
    END_OFF = A.off
    A.seek(WC_OFF)
    X1p = A.alloc([128, 256, 128], BF16)
    assert A.off <= END_OFF

    banks = [es.enter_context(nc.psum_tensor("bank%d" % i, [128, 512], F32)) for i in range(8)]

    def bk(i):
        return banks[i][:]

    def bkb(i):
        return banks[i][:].bitcast(BF16)

    def mm(out, lhsT, rhs, start, stop, reads, writes):
        S.op('pe', lambda e: e.matmul(out, lhsT=lhsT, rhs=rhs, start=start, stop=stop),
             reads=reads, writes=writes)

    def tp(out, in_, reads, writes):
        S.op('pe', lambda e: e.transpose(out=out, in_=in_, identity=ident),
             reads=list(reads) + ['ident'], writes=writes, kind='T')

    def act(out, in_, func, reads, writes, scale=None, bias=None, accum=None):
        kw = {}
        if scale is not None:
            kw['scale'] = scale
        if bias is not None:
            kw['bias'] = bias
        if accum is not None:
            kw['accum_out'] = accum
        S.op('act', lambda e: e.activation(out=out, in_=in_, func=func, **kw),
             reads=reads, writes=writes)

    def tt(eng, out, in0, in1, op, reads, writes):
        S.op(eng, lambda e: e.tensor_tensor(out=out, in0=in0, in1=in1, op=op),
             reads=reads, writes=writes)

    def ts(eng, out, in0, s1, s2, op0, op1, reads, writes):
        if op1 is None:
            S.op(eng, lambda e: e.tensor_scalar(out=out, in0=in0, scalar1=s1, scalar2=None, op0=op0),
                 reads=reads, writes=writes)
        else:
            S.op(eng, lambda e: e.tensor_scalar(out=out, in0=in0, scalar1=s1, scalar2=s2, op0=op0, op1=op1),
                 reads=reads, writes=writes)

    def stt(out, in0, scalar, in1, op0, op1, reads, writes):
        S.op('dve', lambda e: e.scalar_tensor_tensor(out=out, in0=in0, scalar=scalar, in1=in1, op0=op0, op1=op1),
             reads=reads, writes=writes)

    def cp(eng, out, in_, reads, writes):
        if eng == 'act':
            act(out, in_, AF.Copy, reads, writes)
        else:
            S.op(eng, lambda e: e.tensor_copy(out=out, in_=in_), reads=reads, writes=writes)

    def rsqrt_mean(ss, n, tag, width):
        ts('dve', ss, ss, 1.0 / n, EPS, ALU.mult, ALU.add, [tag], [tag])
        tt('pool', ss, ss, mhalf[:, 0:width], ALU.pow, [tag, 'mhalf'], [tag])

    S.dma('sp', 'c0', ident, ident_in[:, :], writes=['ident'])
    S.dma('sp', 'c1', csblk, csblk_in.rearrange("(k p) n -> p k n", p=128), writes=['csblk'])
    S.dma('sp', 'c2', r1[0:K2, :], r1_in[:, :], writes=['r1'])
    S.dma('sp', 'c3', maskP, maskP_in[:, :], writes=['maskP'])
    S.dma('sp', 'c4', maskN, maskN_in[:, :], writes=['maskN'])
    S.dma('sp', 'c5', c256, c256_in[:, :, :, :], writes=['c256'])
    S.dma('sp', 'c6', scol, ccol_in[:, :, :], writes=['scol'])
    S.op('pool', lambda e: e.memset(mhalf, -0.5), writes=['mhalf'])
    for i in range(2):
        S.op('pool', lambda e, i=i: e.memset(Vst[i], 1.0), writes=['Vst%d' % i])
    S.op('pool', lambda e: e.memset(Vc, 1.0), writes=['Vc'])
    act(scol, scol, AF.Silu, ['scol'], ['scol'])

    def layer(l):
        last = (l == NL - 1)
        xsrc = x_in if l == 0 else y
        csrc = ctx_in if l == 0 else xc_s
        S.barrier()
        TA = Arena(arena_t, AR_ELEMS)
        TA.seek(T_OFF)
        S.dma('sp', 'c0', gq, qg_in[l, :].partition_broadcast(128), writes=['gq'])
        S.dma('sp', 'c1', gk, kg_in[l, :].partition_broadcast(128), writes=['gk'])
        S.dma('sp', 'c2', expsink, sink_in[l, :].partition_broadcast(128), writes=['expsink'])
        S.dma('sp', 'c3', sgub, sgu_bT[l, :, :], writes=['sgub'])
        wsT = TA.alloc([128, 4, 128], BF16)
        wsf = TA.alloc([128, 4, 128], F32)
        wsb = TA.alloc([128, 4, 128], BF16)
        for g in range(4):
            S.dma('sp', 'c4', wsf[:, g, :], sgu_w_in[l, g, :, :], writes=['wsf'])
        cp('dve', wsb.rearrange("p a b -> p (a b)"), wsf.rearrange("p a b -> p (a b)"), ['wsf'], ['wsb'])
        for g in range(4):
            tp(bkb(7)[:, g * 128:(g + 1) * 128], wsb[:, g, :], ['wsb'], ['b7'])
        cp('dve', wsT.rearrange("p a b -> p (a b)"), bkb(7)[:, 0:512], ['b7'], ['wsT'])
        S.dma('sp', 'c5', bcol, b_col[l, :, :], writes=['bcol'])
        act(expsink, expsink, AF.Exp, ['expsink'], ['expsink'])
        WA_ = Arena(arena_t, AR_ELEMS)
        WA_.seek(WC_OFF)
        wad = [WA_.alloc([128, 8, 512], F32) for _ in range(2)]
        wft = WA_.alloc([128, 2, 1024], BF16)
        for jc in range(6):
            wb = wad[jc % 2]
            wn = 'wad%d' % (jc % 2)
            S.dma('sp', wn, wb, w_ada[l, :, jc * 512:(jc + 1) * 512].rearrange("(k p) n -> p k n", p=128),
                  writes=[wn])
            for m in range(4):
                col = (jc * 4 + m) * 2
                for k in range(8):
                    mm(bk(0)[:, col:col + 2], wb[:, k, m * 128:(m + 1) * 128], scol[:, k, :],
                       k == 0, k == 7, [wn, 'scol'], ['b0'])
        tt('dve', modsb, bk(0)[:, 0:48].rearrange("p (a b) -> p a b", b=2),
           bcol.unsqueeze(2).to_broadcast([128, 24, 2]), ALU.add, ['b0', 'bcol'], ['modsb'])
        ts('dve', modsb[:, 8:16, :], modsb[:, 8:16, :], 1.0, None, ALU.add, None, ['modsb'], ['modsb'])
        ts('dve', modsb[:, 16:24, :], modsb[:, 16:24, :], 0.5, None, ALU.mult, None, ['modsb'], ['modsb'])
        for r in range(2):
            S.dma('sp', 'gs%d' % r, gate_s[r, :].rearrange("(k p) -> p k", p=128), modsb[:, 16:24, r],
                  reads=['modsb'], writes=['gate_s%d' % r], allow_slow_non_contiguous=True)
        if STOP == 'adaln':
            return True
        for k in range(8):
            S.dma('pool', 'wl%d' % (k % 4), wA[:, k, 512:2816], w_inA[l, k * 128:(k + 1) * 128, :],
                  writes=['wA'])
        for k in range(2):
            S.dma('pool', 'wl%d' % k, wft[:, k, :], w_faT[l, k * 128:(k + 1) * 128, :], writes=['wft'])
        for dk in range(8):
            b = 1 + dk % 2
            for k in range(2):
                mm(bk(b), wft[:, k, dk * 128:(dk + 1) * 128], csblk[:, k, :], k == 0, k == 1,
                   ['wft', 'csblk'], ['b%d' % b])
            cp('act' if dk % 2 else 'dve', wA[:, dk, 0:512], bk(b), ['b%d' % b], ['wA'])

        xt = [TA.alloc([128, 1024], F32) for _ in range(2)]
        junk = TA.alloc([128, 1024], BF16)
        nb = TA.alloc([128, 1024], BF16)
        hT = [TA.alloc([128, 8, 128], BF16) for _ in range(2)]
        ss = TA.alloc([128, 16], F32)
        faT_sb = TA.alloc([128, 4, 128], BF16)
        qgf = TA.alloc([128, 640], F32)
        sqf = TA.alloc([128, 640], F32)
        t1f = TA.alloc([128, 640], F32)
        t2f = TA.alloc([128, 640], F32)
        qrb = TA.alloc([128, 640], BF16)
        QT_sb = TA.alloc([128, 512], BF16)
        KT_sb = TA.alloc([128, 128], BF16)
        vn = TA.alloc([128, 256], BF16)
        mixTb = TA.alloc([128, 512], BF16)
        szb = TA.alloc([128, 256], F32)
        uz = TA.alloc([128, 256], F32)
        sza = TA.alloc([128, 512], BF16)
        ybb = sza[:, 256:512]
        szc = TA.alloc([128, 512], BF16)
        rC = [TA.alloc([128, 64], F32) for _ in range(2)]
        rS = [TA.alloc([128, 64], F32) for _ in range(2)]
        assert TA.off <= T_OFF + T_SIZE, ("passA transients", TA.off - T_OFF)

        def passA(t, isctx, phase):
            r = 1 if isctx else 0
            ti = t - NT if isctx else t
            src = csrc if isctx else xsrc
            s = t % 2
            xn, hn = 'xt%d' % s, 'hT%d' % s
            x_t, h_t = xt[s], hT[s]
            if phase == 'loads':
                S.dma('sp', xn, x_t, src[ti * 128:(ti + 1) * 128, :], reads=['xres%d' % t], writes=[xn])
                if not isctx:
                    S.dma('sp', 'rC%d' % s, rC[s], ropeC_in[ti * 128:(ti + 1) * 128, :], writes=['rC%d' % s])
                    S.dma('sp', 'rS%d' % s, rS[s], ropeS_in[ti * 128:(ti + 1) * 128, :], writes=['rS%d' % s])
                return
            if phase == 'x1':
                for a in range(2):
                    S.dma('sp', 'x1l%d' % a, X1p[a * NT + ti:a * NT + ti + 1, :, :],
                          faT_s[a * 256:(a + 1) * 256, ti * 128:(ti + 1) * 128].unsqueeze(0),
                          reads=['faT_s%d' % ti], writes=['X1p', 'wad0', 'wad1', 'wft'])
                return
            act(junk, x_t, AF.Square, [xn], ['junk', 'ss0'], accum=ss[:, 0:1])
            ts('dve', ss[:, 0:1], ss[:, 0:1], 1.0 / D, EPS, ALU.mult, ALU.add, ['ss0'], ['ss0'])
            tt('pool', ss[:, 0:1], ss[:, 0:1], mhalf[:, 0:1], ALU.pow, ['ss0', 'mhalf'], ['ss0'])
            act(nb, x_t, AF.Identity, [xn, 'ss0'], ['nb'], scale=ss[:, 0:1])
            for k in range(8):
                tp(bkb(0)[:, k * 128:(k + 1) * 128], nb[:, k * 128:(k + 1) * 128], ['nb'], ['b0'])
            for k in range(8):
                if k % 2 == 0:
                    ts('dve', h_t[:, k, :], bkb(0)[:, k * 128:(k + 1) * 128], modsb[:, 8 + k, r:r + 1],
                       modsb[:, k, r:r + 1], ALU.mult, ALU.add, ['b0', 'modsb'], [hn])
                else:
                    act(h_t[:, k, :], bkb(0)[:, k * 128:(k + 1) * 128], AF.Identity, ['b0', 'modsb'], [hn],
                        scale=modsb[:, 8 + k, r:r + 1], bias=modsb[:, k, r:r + 1])
            only_kv = isctx and last
            if not only_kv:
                S.dma('act', 'hTs%d' % s, hT_s[t, :, :], h_t.rearrange("p k n -> p (k n)"), reads=[hn],
                      writes=['hT_s%d' % t])
            if not only_kv:
                if not isctx:
                    for c4 in range(4):
                        for k in range(8):
                            mm(bk(1)[:, c4 * 128:(c4 + 1) * 128], wA[:, k, c4 * 128:(c4 + 1) * 128], h_t[:, k, :],
                               k == 0, k == 7, ['wA', hn], ['b1'])
                    cp('act', faT_sb.rearrange("p a b -> p (a b)"), bk(1), ['b1'], ['faT_sb'])
                    S.dma('act', 'faTs', faT_s.rearrange("(c p) n -> p c n", p=128)[:, :, ti * 128:(ti + 1) * 128],
                          faT_sb, reads=['faT_sb'], writes=['faT_s%d' % ti])
                else:
                    for k in range(8):
                        mm(bk(1), h_t[:, k, :], wA[:, k, 0:512], k == 0, k == 7, ['wA', hn], ['b1'])
                    cp('act', fac[:, ti, :], bk(1), ['b1'], ['fac'])
            def proj(bi, c0, c1):
                for k in range(8):
                    mm(bk(bi)[:, 0:c1 - c0], h_t[:, k, :], wA[:, k, c0:c1], k == 0, k == 7, ['wA', hn], ['b%d' % bi])
            if not only_kv:
                proj(2, 512, 1024)
            proj(3, 1024, 1536)
            if not only_kv:
                proj(4, 1536, 2048)
                proj(5, 2048, 2560)
                proj(6, 2560, 2816)
            def qk(ps, c0, nh, gain, gname, bn):
                w = nh * 64
                v3 = lambda ap: ap[:, c0:c0 + w].rearrange("p (h d) -> p h d", d=64)
                tt('dve', v3(qgf), ps.rearrange("p (h d) -> p h d", d=64),
                   gain.unsqueeze(1).to_broadcast([128, nh, 64]), ALU.mult, [bn, gname], ['qgf'])
                act(sqf[:, c0:c0 + w], ps, AF.Square, [bn], ['sqf'])
                sname = 'ssq%d' % c0
                sv = ss[:, 2:2 + nh] if c0 == 0 else ss[:, 12:12 + nh]
                S.op('dve', lambda e: e.tensor_reduce(out=sv, in_=v3(sqf), axis=AX.X, op=ALU.add),
                     reads=['sqf'], writes=[sname])
                ts('dve', sv, sv, 1.0 / 64, EPS, ALU.mult, ALU.add, [sname], [sname])
                tt('pool', sv, sv, mhalf[:, 0:nh], ALU.pow, [sname, 'mhalf'], [sname])
                if not isctx:
                    cb = rC[s].unsqueeze(1).to_broadcast([128, nh, 64])
                    tt('pool', v3(t1f), v3(qgf), cb, ALU.mult, ['qgf', 'rC%d' % s], ['t1f'])
                    v5 = lambda ap: ap[:, c0:c0 + w].rearrange("p (h b i d) -> p h b i d", b=2, i=2, d=16)
                    s5 = rS[s].rearrange("p (b i d) -> p b i d", b=2, i=2)
                    for i in range(2):
                        tt('pool', v5(t2f)[:, :, :, i, :], v5(qgf)[:, :, :, 1 - i, :],
                           s5[:, :, i, :].unsqueeze(1).to_broadcast([128, nh, 2, 16]), ALU.mult,
                           ['qgf', 'rS%d' % s], ['t2f'])
                    tt('pool', v3(t1f), v3(t1f), v3(t2f), ALU.add, ['t1f', 't2f'], ['t1f'])
                    srcq, srcn = t1f, 't1f'
                else:
                    srcq, srcn = qgf, 'qgf'
                tt('dve', v3(qrb), v3(srcq), sv.unsqueeze(2).to_broadcast([128, nh, 64]), ALU.mult,
                   [srcn, sname], ['qrb'])
            if not only_kv:
                qk(bk(2), 0, 8, gq, 'gq', 'b2')
                for j in range(4):
                    tp(bkb(7)[:, j * 128:(j + 1) * 128], qrb[:, j * 128:(j + 1) * 128], ['qrb'], ['b7'])
                cp('act', QT_sb, bkb(7)[:, 0:512], ['b7'], ['QT_sb'])
                S.dma('act', 'QTs', QT_s[t, :, :], QT_sb, reads=['QT_sb'], writes=['QT_s%d' % t])
            qk(bk(3)[:, 0:128], 512, 2, gk, 'gk', 'b3')
            tp(bkb(7)[:, 512:640], qrb[:, 512:640], ['qrb'], ['b7'])
            if isctx:
                cp('dve', KcT[:, ti * 128:(ti + 1) * 128], bkb(7)[:, 512:640], ['b7'], ['KcT'])
                cp('act', Vc[:, ti, :, 0:64], bk(3)[:, 128:256].rearrange("p (h d) -> p h d", d=64), ['b3'], ['Vc'])
            else:
                cp('act', KT_sb, bkb(7)[:, 512:640], ['b7'], ['KT_sb'])
                S.dma('act', 'KTs', KT_s[:, ti * 128:(ti + 1) * 128], KT_sb, reads=['KT_sb'], writes=['KT_s%d' % ti])
                vs_ = Vst[s]
                cp('act', vs_[:, :, 0:64], bk(3)[:, 128:256].rearrange("p (h d) -> p h d", d=64), ['b3'],
                   ['Vst%d' % s])
                S.dma('act', 'Vs%d' % s, V_s[ti * 128:(ti + 1) * 128, :], vs_.rearrange("p a b -> p (a b)"),
                      reads=['Vst%d' % s], writes=['V_s%d' % ti])
            if only_kv:
                return
            act(junk[:, 0:256], bk(3)[:, 256:512], AF.Square, ['b3'], ['junk', 'ss1'], accum=ss[:, 1:2])
            ts('dve', ss[:, 1:2], ss[:, 1:2], 1.0 / 256, EPS, ALU.mult, ALU.add, ['ss1'], ['ss1'])
            tt('pool', ss[:, 1:2], ss[:, 1:2], mhalf[:, 0:1], ALU.pow, ['ss1', 'mhalf'], ['ss1'])
            act(vn, bk(3)[:, 256:512], AF.Identity, ['b3', 'ss1'], ['vn'], scale=ss[:, 1:2])
            for g in range(4):
                mm(bk(1)[:, g * 128:(g + 1) * 128], wsT[:, g, :], vn[:, (g // 2) * 128:(g // 2) * 128 + 128], True, True,
                   ['wsT', 'vn'], ['b1'])
            act(szb, bk(4)[:, 256:512], AF.Silu, ['b4'], ['szb'])
            tt('dve', uz, bk(4)[:, 0:256], szb, ALU.mult, ['b4', 'szb'], ['uz'])
            for g in range(4):
                stt(ybb[:, g * 64:(g + 1) * 64], bk(1)[:, g * 128 + (g % 2) * 64:g * 128 + (g % 2) * 64 + 64], sgub[:, g:g + 1],
                    uz[:, g * 64:(g + 1) * 64], ALU.add, ALU.mult, ['b1', 'uz', 'sgub'], ['ybb'])
            act(sza[:, 0:256], bk(6)[:, 0:256], AF.Silu, ['b6'], ['sza'])
            S.dma('act', 'szas', sza_s[t * 128:(t + 1) * 128, :], sza, reads=['sza', 'ybb'], writes=['sza_s%d' % t])
            act(szc, bk(5), AF.Silu, ['b5'], ['szc'])
            S.dma('act', 'szcs', szc_s[t * 128:(t + 1) * 128, :], szc, reads=['szc'], writes=['szc_s%d' % t])

        if STOP == 'dbg_ws':
            S.dma('sp', 'c0', ybT_s[NT, :, :], wsT.rearrange("p a b -> p (a b)")[:, 0:256], reads=['wsT'], writes=['dbgx'])
        if STOP == 'wA':
            return True
        seqA = [(t, True) for t in range(NT, NTT)] + [(t, False) for t in range(NT)]
        passA(seqA[0][0], seqA[0][1], 'loads')
        for i, (t, ic) in enumerate(seqA):
            if i + 1 < len(seqA):
                passA(seqA[i + 1][0], seqA[i + 1][1], 'loads')
            passA(t, ic, 'compute')
            if i >= 1 and not seqA[i - 1][1]:
                passA(seqA[i - 1][0], False, 'x1')
        passA(seqA[-1][0], False, 'x1')
        if STOP == 'passA':
            return True

        S.barrier()
        FA = Arena(arena_t, AR_ELEMS)
        FA.seek(WA_OFF)
        Gc = [FA.alloc([128, K2, 64], BF16) for _ in range(2)]
        Yc = [FA.alloc([128, NT, 64], BF16) for _ in range(2)]
        KG = min(8, NT)
        Mst = [FA.alloc([128, KG, 2, 128], BF16) for _ in range(2)]
        Yct = FA.alloc([128, 256], BF16)
        assert FA.off <= WC_OFF
        ysc = 1.0 / np.sqrt(float(L))
        if not last:
            for kt in range(2):
                i = 0
                for nt in range(2):
                    for a in range(2):
                        mm(bk(7)[:, 0:256], c256[:, nt, a, kt * 128:(kt + 1) * 128], fac[:, nt, a * 256:(a + 1) * 256],
                           i == 0, i == 3, ['c256', 'fac'], ['b7'])
                        i += 1
                act(Yct, bk(7)[:, 0:256], AF.Copy, ['b7'], ['Yct'], scale=1.0 / 16.0)
                S.dma('sp', 'Ycts', Y_s[:, L + kt * 128:L + (kt + 1) * 128, :].rearrange("c n e -> n c e"),
                      Yct.rearrange("p (c e) -> p c e", c=4), reads=['Yct'], writes=['Y_sc%d' % kt])
        mcnt = 0
        for c in range(4):
            g_t = Gc[c % 2]
            gn = 'Gc%d' % (c % 2)
            y_t = Yc[c % 2]
            yn = 'Yc%d' % (c % 2)
            for q4 in range(16):
                b = q4 % 2
                for i in range(4):
                    ch = c * 64 + q4 * 4 + i
                    mm(bk(b)[:, i * K2:(i + 1) * K2], X1p[0:K2, ch, :], r1[0:K2, :], True, True,
                       ['X1p', 'r1'], ['b%d' % b])
                cp('act' if q4 % 2 else 'dve',
                   g_t[:, :, q4 * 4:q4 * 4 + 4].rearrange("p j c -> p c j"),
                   bk(b)[:, 0:4 * K2].rearrange("p (c j) -> p c j", c=4), ['b%d' % b], [gn])
            for kg in range(NT // KG):
                m_t = Mst[mcnt % 2]
                mn = 'Mst%d' % (mcnt % 2)
                mcnt += 1
                S.dma('sp', mn, m_t, m2_in[:, kg * KG:(kg + 1) * KG, :, :], writes=[mn])
                b = 2 + kg % 2
                for kk in range(KG):
                    k2 = kg * KG + kk
                    for r_ in range(2):
                        mm(bk(b)[:, kk * 64:(kk + 1) * 64], m_t[:, kk, r_, :], g_t[:, 2 * k2 + r_, :],
                           r_ == 0, r_ == 1, [mn, gn], ['b%d' % b])
                act(y_t[:, kg * KG:(kg + 1) * KG, :].rearrange("p a b -> p (a b)"), bk(b)[:, 0:KG * 64], AF.Copy,
                    ['b%d' % b], [yn], scale=float(ysc))
            S.dma('sp', yn + 's', Y_s[c, 0:L, :].rearrange("(k1 k2) e -> k1 k2 e", k2=NT), y_t,
                  reads=[yn], writes=['Y_s'])

        if STOP == 'fourier':
            return True
        S.barrier()
        for k in range(8):
            S.dma('pool', 'wl%d' % (k % 4), wg[:, k, :], w_g[l, k * 128:(k + 1) * 128, :], writes=['wg'])
        for k in range(2):
            S.dma('pool', 'wl%d' % k, wpa[:, k, :], w_pa[l, k * 128:(k + 1) * 128, :], writes=['wpa'])
            S.dma('pool', 'wl%d' % (2 + k), wpb[:, k, :], w_pb[l, k * 128:(k + 1) * 128, :], writes=['wpb'])
        for k in range(4):
            S.dma('pool', 'wl%d' % k, wpc[:, k, :], w_pc[l, k * 128:(k + 1) * 128, :], writes=['wpc'])
        for k in range(8):
            S.dma('pool', 'wl%d' % (k % 4), wout[:, k, :], w_out[l, k * 128:(k + 1) * 128, :], writes=['wout'])
        CA = Arena(arena_t, AR_ELEMS)
        CA.seek(WA_OFF)
        gate_bc = [CA.alloc([128, 1024], F32) for _ in range(2)]
        for r in range(2):
            S.dma('sp', 'gb%d' % r, gate_bc[r], gate_s[r, :].partition_broadcast(128), reads=['gate_s%d' % r],
                  writes=['gate_bc%d' % r])
        def two(shape, dt):
            return [CA.alloc(shape, dt) for _ in range(2)]
        xt = two([128, 1024], F32)
        hT = two([128, 8, 128], BF16)
        QTt = two([128, 512], BF16)
        KTw = two([128, 3, 128], BF16)
        Vw = two([128, 3, 130], BF16)
        szc = two([128, 512], BF16)
        sza = two([128, 512], BF16)
        Yt = two([128, 256], BF16)
        E = two([128, 5, 2, 512], BF16)
        tg = two([128, 3072], BF16)
        den = two([128, 8], F32)
        atf = two([128, 512], F32)
        ycb = two([128, 512], BF16)
        yab = two([128, 256], BF16)
        yT = two([128, 8, 128], BF16)
        m1 = two([128, 512], F32)
        m2 = two([128, 512], F32)
        m3 = two([128, 512], F32)
        mg = two([128, 1024], BF16)
        mT = two([128, 8, 128], BF16)
        xo = two([128, 1024], F32)
        assert CA.off <= WC_OFF, ("passC transients", CA.off, WC_OFF)

        def passC_chunks(t, isctx, s):
            r = 1 if isctx else 0
            ti = t - NT if isctx else t
            src = csrc if isctx else xsrc
            dst = xc_s if isctx else y
            B = [4 * s + i for i in range(4)]
            bn = ['b%d' % b for b in B]
            sfx = '%d' % s
            xn, hn = 'xt' + sfx, 'hT' + sfx
            R = lambda nm: nm + sfx
            kbs = []
            if not isctx:
                lo = max(ti - 1, 0)
                hi = min(ti + 1, NT - 1)
                nb_ = hi - lo + 1
                for u in range(lo, hi + 1):
                    mk = None if u == ti else (maskP if u < ti else maskN)
                    kbs.append((KTw[s][:, u - lo, :], Vw[s][:, u - lo, :].rearrange("p (h e) -> p h e", h=2), mk,
                                [R('KTw')], [R('Vw')]))
            for u in range(CT):
                kbs.append((KcT[:, u * 128:(u + 1) * 128], Vc[:, u, :, :], None, ['KcT'], ['Vc']))
            nk = len(kbs)

            def c_loads():
                S.dma('sp', xn, xt[s], src[ti * 128:(ti + 1) * 128, :], reads=['xres%d' % t], writes=[xn])
                S.dma('sp', hn, hT[s].rearrange("p k n -> p (k n)"), hT_s[t, :, :], reads=['hT_s%d' % t], writes=[hn])
                S.dma('sp', R('QT'), QTt[s], QT_s[t, :, :], reads=['QT_s%d' % t], writes=[R('QT')])
                S.dma('sp', R('szc'), szc[s], szc_s[t * 128:(t + 1) * 128, :], reads=['szc_s%d' % t],
                      writes=[R('szc')])
                S.dma('sp', R('sza'), sza[s], sza_s[t * 128:(t + 1) * 128, :], reads=['sza_s%d' % t],
                      writes=[R('sza')])
                S.dma('sp', R('Yt'), Yt[s].rearrange("p (c e) -> p c e", c=4),
                      Y_s[:, t * 128:(t + 1) * 128, :].rearrange("c n e -> n c e"),
                      reads=['Y_s', 'Y_sc0', 'Y_sc1'], writes=[R('Yt')])
                if not isctx:
                    S.dma('sp', R('KTw'), KTw[s][:, 0:nb_, :].rearrange("p a b -> p (a b)"),
                          KT_s[:, lo * 128:(hi + 1) * 128], reads=['KT_s%d' % u for u in range(lo, hi + 1)],
                          writes=[R('KTw')])
                    S.dma('sp', R('Vw'), Vw[s][:, 0:nb_, :],
                          V_s[lo * 128:(hi + 1) * 128, :].rearrange("(a p) e -> p a e", p=128),
                          reads=['V_s%d' % u for u in range(lo, hi + 1)], writes=[R('Vw')])

            def c_score(bi):
                kT, vv, mk, kr, vr = kbs[bi]
                for kv in range(2):
                    b = B[kv]
                    mm(bk(b), kT[kv * 64:(kv + 1) * 64, :], QTt[s][kv * 64:(kv + 1) * 64, :], True, mk is None,
                       kr + [R('QT')], [bn[kv]])
                    if mk is not None:
                        mm(bk(b), ident, mk, False, True, ['ident', 'maskP', 'maskN'], [bn[kv]])
                for kv in range(2):
                    act(E[s][:, bi, kv, :], bk(B[kv]), AF.Exp, [bn[kv]], [R('E')], scale=0.125)

            def c_gate(c):
                b = B[2 + c % 2]
                for k in range(8):
                    mm(bk(b), hT[s][:, k, :], wg[:, k, c * 512:(c + 1) * 512], k == 0, k == 7, ['wg', hn],
                       [bn[2 + c % 2]])
                act(tg[s][:, c * 512:(c + 1) * 512], bk(b), AF.Tanh, [bn[2 + c % 2]], [R('tg')], scale=0.5)

            def c_pv(kv):
                ob = bk(B[2 + kv])[:, 0:260].rearrange("p (j e) -> p j e", e=65)
                for j in range(4):
                    for bi, (kT, vv, mk, kr, vr) in enumerate(kbs):
                        mm(ob[:, j, :], E[s][:, bi, kv, j * 128:(j + 1) * 128], vv[:, kv, :], bi == 0,
                           bi == nk - 1, [R('E')] + vr, [bn[2 + kv]])
                tt('dve', den[s][:, kv * 4:(kv + 1) * 4], ob[:, :, 64], expsink[:, kv * 4:(kv + 1) * 4], ALU.add,
                   [bn[2 + kv], 'expsink'], [R('den')])

            def c_norm():
                S.op('dve', lambda e: e.reciprocal(out=den[s], in_=den[s]), reads=[R('den')], writes=[R('den')])
                for kv in range(2):
                    ob = bk(B[2 + kv])[:, 0:260].rearrange("p (j e) -> p j e", e=65)
                    tt('dve', atf[s][:, kv * 256:(kv + 1) * 256].rearrange("p (j d) -> p j d", d=64), ob[:, :, 0:64],
                       den[s][:, kv * 4:(kv + 1) * 4].unsqueeze(2).to_broadcast([128, 4, 64]), ALU.mult,
                       [bn[2 + kv], R('den')], [R('atf')])
                tt('pool', ycb[s], atf[s], szc[s], ALU.mult, [R('atf'), R('szc')], [R('ycb')])
                tt('pool', yab[s], Yt[s], sza[s][:, 0:256], ALU.mult, [R('Yt'), R('sza')], [R('yab')])

            def c_tr():
                tb = bkb(B[0])
                for j in range(2):
                    tp(tb[:, j * 128:(j + 1) * 128], yab[s][:, j * 128:(j + 1) * 128], [R('yab')], [bn[0]])
                for j in range(4):
                    tp(tb[:, (2 + j) * 128:(3 + j) * 128], ycb[s][:, j * 128:(j + 1) * 128], [R('ycb')], [bn[0]])
                for j in range(2):
                    tp(tb[:, (6 + j) * 128:(7 + j) * 128], sza[s][:, 256 + j * 128:256 + (j + 1) * 128],
                       [R('sza')], [bn[0]])
                cp('act', yT[s].rearrange("p a b -> p (a b)"), tb, [bn[0]], [R('yT')])

            def c_proj(hD):
                cs = slice(hD * 512, (hD + 1) * 512)
                for k in range(2):
                    mm(bk(B[1]), yT[s][:, k, :], wpa[:, k, cs], k == 0, k == 1, [R('yT'), 'wpa'], [bn[1]])
                for k in range(2):
                    mm(bk(B[2]), yT[s][:, 6 + k, :], wpb[:, k, cs], k == 0, k == 1, [R('yT'), 'wpb'], [bn[2]])
                for k in range(4):
                    mm(bk(B[3]), yT[s][:, 2 + k, :], wpc[:, k, cs], k == 0, k == 3, [R('yT'), 'wpc'], [bn[3]])
                stt(m1[s], tg[s][:, hD * 512:(hD + 1) * 512], 1.0, bk(B[1]), ALU.add, ALU.mult, [R('tg'), bn[1]],
                    [R('m1')])
                stt(m2[s], tg[s][:, 1024 + hD * 512:1024 + (hD + 1) * 512], 1.0, bk(B[2]), ALU.add, ALU.mult,
                    [R('tg'), bn[2]], [R('m2')])
                stt(m3[s], tg[s][:, 2048 + hD * 512:2048 + (hD + 1) * 512], 1.0, bk(B[3]), ALU.add, ALU.mult,
                    [R('tg'), bn[3]], [R('m3')])
                tt('pool', m1[s], m1[s], m2[s], ALU.add, [R('m1'), R('m2')], [R('m1')])
                tt('pool', mg[s][:, cs], m1[s], m3[s], ALU.add, [R('m1'), R('m3')], [R('mg')])

            def c_mt():
                tb = bkb(B[0])
                for k in range(8):
                    tp(tb[:, k * 128:(k + 1) * 128], mg[s][:, k * 128:(k + 1) * 128], [R('mg')], [bn[0]])
                cp('act', mT[s].rearrange("p a b -> p (a b)"), tb, [bn[0]], [R('mT')])

            def c_out(hD):
                cs = slice(hD * 512, (hD + 1) * 512)
                b = B[1 + hD]
                for k in range(8):
                    mm(bk(b), mT[s][:, k, :], wout[:, k, cs], k == 0, k == 7, [R('mT'), 'wout'], [bn[1 + hD]])
                tt('dve', xo[s][:, cs], bk(b), gate_bc[r][:, cs], ALU.mult, [bn[1 + hD], 'gate_bc%d' % r],
                   [R('xo')])

            def c_fin():
                tt('pool', xo[s], xo[s], xt[s], ALU.add, [R('xo'), xn], [R('xo')])
                S.dma('pool', R('xo'), dst[ti * 128:(ti + 1) * 128, :], xo[s], reads=[R('xo')],
                      writes=['xres%d' % t])

            ch = [c_loads]
            sc = [lambda bi=bi: c_score(bi) for bi in range(nk)]
            gg = [lambda c=c: c_gate(c) for c in range(6)]
            while sc or gg:
                if sc:
                    ch.append(sc.pop(0))
                if gg:
                    ch.append(gg.pop(0))
            ch += [lambda: c_pv(0), lambda: c_pv(1), c_norm, c_tr, lambda: c_proj(0), lambda: c_proj(1), c_mt,
                   lambda: c_out(0), lambda: c_out(1), c_fin]
            return ch

        seqC = ([(t, True) for t in range(NT, NTT)] if not last else []) + [(t, False) for t in range(NT)]
        lists = [passC_chunks(t, ic, i % 2) for i, (t, ic) in enumerate(seqC)]
        H = (max(len(c) for c in lists) + 1) // 2
        items = sorted((i * H + j, i, j) for i, c in enumerate(lists) for j in range(len(c)))
        for _, i, j in items:
            lists[i][j]()

    for l in range(NL):
        if layer(l):
            break
    S.finish('sp')
    es.close()
    return nc


def _bf(a):
    return np.ascontiguousarray(a.astype(ml_dtypes.bfloat16))


def _tables(NT):
    L = NT * 128
    K2 = 2 * NT
    t = {}
    t['ident'] = _bf(np.eye(128, dtype=np.float32))
    j = np.arange(64)
    ang = 2 * np.pi * np.outer(j, j) / 64.0
    cs = np.zeros((256, 512), np.float64)
    for g in range(4):
        cs[g * 64:(g + 1) * 64, g * 64:(g + 1) * 64] = np.cos(ang) / 8.0
        cs[g * 64:(g + 1) * 64, 256 + g * 64:256 + (g + 1) * 64] = np.sin(ang) / 8.0
    t['csblk'] = _bf(cs)
    n2 = np.arange(NT)
    th = 2 * np.pi * np.outer(n2, n2) / NT
    r1 = np.zeros((2, NT, NT, 2), np.float64)
    r1[0, :, :, 0] = np.cos(th)
    r1[1, :, :, 0] = -np.sin(th)
    r1[0, :, :, 1] = np.sin(th)
    r1[1, :, :, 1] = np.cos(th)
    t['r1'] = _bf(r1.reshape(K2, K2))
    n1 = np.arange(128)[:, None, None]
    k2 = np.arange(NT)[None, :, None]
    k1 = np.arange(128)[None, None, :]
    kk = (NT * k1 + k2).astype(np.int64)
    ph = 2 * np.pi * ((n1 * kk) % L) / L
    m2 = np.stack([np.cos(ph), -np.sin(ph)], axis=2)
    t['m2'] = _bf(m2)
    n = (np.arange(2)[None, :, None] * 128 + np.arange(128)[:, None, None])
    k = np.arange(256)[None, None, :]
    ph = 2 * np.pi * ((n * k) % 256) / 256.0
    t['c256'] = _bf(np.stack([np.cos(ph), -np.sin(ph)], axis=2))
    pos = np.arange(L)
    row = (pos // 64).astype(np.float32)
    col = (pos % 64).astype(np.float32)
    inv = (10000.0 ** (-np.arange(16, dtype=np.float32) / 16)).astype(np.float32)
    ar = row[:, None] * inv[None, :]
    ac = col[:, None] * inv[None, :]
    t['ropeC'] = np.concatenate([np.cos(ar), np.cos(ar), np.cos(ac), np.cos(ac)], axis=1).astype(np.float32)
    t['ropeS'] = np.concatenate([-np.sin(ar), np.sin(ar), -np.sin(ac), np.sin(ac)], axis=1).astype(np.float32)
    p = np.arange(128)[:, None]
    f = (np.arange(512) % 128)[None, :]
    t['maskP'] = _bf(np.where(p >= f, 0.0, NEG))
    t['maskN'] = _bf(np.where(p <= f, 0.0, NEG))
    return t


def _prep_shared(NL, w_ada, b_ada, w_in, sgu_w, sgu_b, q_norm_g, k_norm_g, attn_sink, w_pa, w_pb, w_pc, w_out):
    f = lambda a: np.ascontiguousarray(np.asarray(a, dtype=np.float32))
    w_in = np.asarray(w_in, dtype=np.float32)
    qperm = []
    for j in range(4):
        for kv in range(2):
            h = kv * 4 + j
            qperm.extend(range(1280 + h * 64, 1280 + (h + 1) * 64))
    colsA = (qperm + list(range(1792, 1920)) + list(range(1920, 2048)) + list(range(768, 1024))
             + list(range(512, 768)) + list(range(1024, 1280)) + list(range(2048, 2560)) + list(range(256, 512)))
    sh = {}
    sh['w_ada'] = f(w_ada[:NL])
    b = np.asarray(b_ada, dtype=np.float32)[:NL]
    sh['b_col'] = f(b.reshape(NL, 24, 128).transpose(0, 2, 1))
    sh['w_inA'] = f(w_in[:NL][:, :, colsA])
    sh['w_faT'] = f(w_in[:NL][:, :, 0:256].transpose(0, 2, 1))
    sh['w_g'] = f(w_in[:NL][:, :, 2560:5632])
    sh['w_pa'] = f(w_pa[:NL])
    sh['w_pb'] = f(w_pb[:NL])
    sh['w_pc'] = f(w_pc[:NL])
    sh['w_out'] = f(w_out[:NL])
    sh['sgu_w'] = f(sgu_w[:NL])
    sh['sgu_bT'] = f(np.asarray(sgu_b, dtype=np.float32)[:NL].transpose(0, 2, 1))
    sh['qg'] = f(q_norm_g[:NL])
    sh['kg'] = f(k_norm_g[:NL])
    sh['sink'] = f(attn_sink[:NL])
    return sh


_CACHE = {}


def run(x, c, ctx, c_ctx, NL, **params):
    x = np.asarray(x, dtype=np.float32)
    B, L, _ = x.shape
    NT = L // 128
    key = (NT, NL)
    if key not in _CACHE:
        _CACHE[key] = build(NT, NL)
    nc = _CACHE[key]
    shared = _prep_shared(NL, **params)
    shared.update(_tables(NT))
    c = np.asarray(c, dtype=np.float32)
    c_ctx = np.asarray(c_ctx, dtype=np.float32)
    ctx = np.asarray(ctx, dtype=np.float32)
    in_maps = []
    for core in range(8):
        b = core % B
        m = dict(shared)
        m['x'] = np.ascontiguousarray(x[b])
        m['ctx'] = np.ascontiguousarray(ctx[b])
        cc = np.stack([c[b].reshape(8, 128).T, c_ctx.reshape(8, 128).T], axis=2)
        m['ccol'] = np.ascontiguousarray(cc.astype(np.float32))
        in_maps.append(m)
    res = run_bass_kernel_spmd(nc, in_maps, core_ids=list(range(8)))
    if DEBUG:
        return res.results
    return np.stack([res.results[b]["y"] for b in range(B)], axis=0).astype(np.float32)


def kernel(x, c, ctx, c_ctx, w_ada, b_ada, w_in, sgu_w, sgu_b, q_norm_g, k_norm_g,
           attn_sink, w_pa, w_pb, w_pc, w_out):
    return run(x, c, ctx, c_ctx, 4, w_ada=w_ada, b_ada=b_ada, w_in=w_in, sgu_w=sgu_w, sgu_b=sgu_b,
               q_norm_g=q_norm_g, k_norm_g=k_norm_g, attn_sink=attn_sink, w_pa=w_pa, w_pb=w_pb,
               w_pc=w_pc, w_out=w_out)
```

```python
import contextlib
import numpy as np
import ml_dtypes
import concourse.bass as bass
import concourse.mybir as mybir
from concourse.bass_utils import run_bass_kernel_spmd

F32 = mybir.dt.float32
BF16 = mybir.dt.bfloat16
AF = mybir.ActivationFunctionType
ALU = mybir.AluOpType
AX = mybir.AxisListType

DEBUG = False
STOP = None
D = 1024
CT = 2
EPS = 1e-6
NEG = -30000.0


class Sched:
    def __init__(self, nc, es):
        self.nc = nc
        self.es = es
        self.eng = {'pe': nc.tensor, 'act': nc.scalar, 'dve': nc.vector,
                    'pool': nc.gpsimd, 'sp': nc.sync}
        self.sem = {}
        self.cnt = {}
        for k in self.eng:
            self.sem[k] = es.enter_context(nc.semaphore("s_" + k))
            self.cnt[k] = 0
        self.lastw = {}
        self.readers = {}
        self.seen = {}
        self.nchan = 0
        self.chans = {}
        self.pe_kind = None

    def chan(self, name):
        if name in self.chans:
            return self.chans[name]
        k = "ch%d" % self.nchan
        self.nchan += 1
        self.sem[k] = self.es.enter_context(self.nc.semaphore("d_" + k))
        self.cnt[k] = 0
        self.chans[name] = k
        return k

    def _need(self, reads, writes):
        need = {}

        def add(x):
            if x is None:
                return
            if need.get(x[0], 0) < x[1]:
                need[x[0]] = x[1]
        for r in reads:
            add(self.lastw.get(r))
        for w in writes:
            add(self.lastw.get(w))
            for rd in self.readers.get(w, ()):
                add(rd)
        return need

    def _waits(self, e, need):
        for k, c in need.items():
            if k == e and e == 'pe':
                continue
            if self.seen.get((e, k), 0) >= c:
                continue
            self.eng[e].wait_ge(self.sem[k], c)
            self.seen[(e, k)] = c

    def _commit(self, key, reads, writes):
        tok = (key, self.cnt[key])
        for r in reads:
            self.readers.setdefault(r, []).append(tok)
        for w in writes:
            self.lastw[w] = tok
            self.readers[w] = []

    def op(self, e, fn, reads=(), writes=(), kind='M'):
        pb = [r for r in reads if len(r) == 2 and r[0] == 'b' and r[1].isdigit()]
        if pb:
            writes = list(writes) + [r for r in pb if r not in writes]
        self._waits(e, self._need(reads, writes))
        if e == 'pe':
            if self.pe_kind is not None and kind != self.pe_kind and self.cnt['pe'] > 0:
                if self.seen.get(('pe', 'pe'), 0) < self.cnt['pe']:
                    self.eng['pe'].wait_ge(self.sem['pe'], self.cnt['pe'])
                    self.seen[('pe', 'pe')] = self.cnt['pe']
            self.pe_kind = kind
        ins = fn(self.eng[e])
        self.cnt[e] += 1
        ins.then_inc(self.sem[e], 1)
        self._commit(e, reads, writes)

    def dma(self, e, chname, out, in_, reads=(), writes=(), **kw):
        ch = self.chan(chname)
        need = self._need(reads, writes)
        if self.cnt[ch] > 0:
            need[ch] = max(need.get(ch, 0), self.cnt[ch])
        self._waits(e, need)
        ins = self.eng[e].dma_start(out=out, in_=in_, **kw)
        self.cnt[ch] += 16
        ins.then_inc(self.sem[ch], 16)
        self._commit(ch, reads, writes)

    def barrier(self):
        for e in self.eng:
            need = {k: c for k, c in self.cnt.items() if c > 0 and k != e}
            self._waits(e, need)

    def finish(self, e='sp'):
        need = {k: c for k, c in self.cnt.items() if c > 0 and k != e}
        self._waits(e, need)


class Arena:
    def __init__(self, tensor, nelem):
        self.t = tensor
        self.n = nelem
        self.off = 0

    def seek(self, off):
        self.off = off

    def alloc(self, shape, dt):
        n = 1
        for s in shape[1:]:
            n *= s
        nb = n * (4 if dt == F32 else 2)
        nb = (nb + 255) // 256 * 256
        ne = nb // 2
        assert self.off + ne <= self.n, ("arena overflow", self.off, ne, self.n)
        ap = self.t[:, self.off:self.off + n * (2 if dt == F32 else 1)]
        self.off += ne
        if dt == F32:
            ap = ap.bitcast(F32)
        if len(shape) == 3:
            ap = ap.rearrange("p (a b) -> p a b", a=shape[1])
        elif len(shape) == 4:
            ap = ap.rearrange("p (a b c) -> p a b c", a=shape[1], b=shape[2])
        return ap


def build(NT, NL):
    L = NT * 128
    NTT = NT + CT
    K2 = 2 * NT
    nc = bass.Bass("TRN2", target_bir_lowering=False)
    es = contextlib.ExitStack()
    S = Sched(nc, es)

    def din(name, shape, dt=F32):
        return nc.dram_tensor(name, shape, dt, kind="ExternalInput").ap()

    def dscr(name, shape, dt):
        return nc.dram_tensor(name, shape, dt, kind="ExternalOutput" if DEBUG else "Internal").ap()

    x_in = din("x", [L, D])
    ctx_in = din("ctx", [CT * 128, D])
    ccol_in = din("ccol", [128, 8, 2])
    w_ada = din("w_ada", [NL, D, 3 * D])
    b_col = din("b_col", [NL, 128, 24])
    w_inA = din("w_inA", [NL, D, 2304])
    w_faT = din("w_faT", [NL, 256, D])
    w_g = din("w_g", [NL, D, 3 * D])
    w_pa = din("w_pa", [NL, 256, D])
    w_pb = din("w_pb", [NL, 256, D])
    w_pc = din("w_pc", [NL, 512, D])
    w_out = din("w_out", [NL, D, D])
    sgu_w_in = din("sgu_w", [NL, 4, 128, 128])
    sgu_bT = din("sgu_bT", [NL, 128, 4])
    qg_in = din("qg", [NL, 64])
    kg_in = din("kg", [NL, 64])
    sink_in = din("sink", [NL, 8])
    ident_in = din("ident", [128, 128], BF16)
    csblk_in = din("csblk", [256, 512], BF16)
    r1_in = din("r1", [K2, K2], BF16)
    m2_in = din("m2", [128, NT, 2, 128], BF16)
    c256_in = din("c256", [128, 2, 2, 256], BF16)
    ropeC_in = din("ropeC", [L, 64])
    ropeS_in = din("ropeS", [L, 64])
    maskP_in = din("maskP", [128, 512], BF16)
    maskN_in = din("maskN", [128, 512], BF16)
    y = nc.dram_tensor("y", [L, D], F32, kind="ExternalOutput").ap()

    xc_s = dscr("xc_s", [CT * 128, D], F32)
    gate_s = dscr("gate_s", [2, D], F32)
    faT_s = dscr("faT_s", [512, L], BF16)
    hT_s = dscr("hT_s", [NTT, 128, D], BF16)
    QT_s = dscr("QT_s", [NTT, 128, 512], BF16)
    KT_s = dscr("KT_s", [128, L], BF16)
    V_s = dscr("V_s", [L, 130], BF16)
    ybT_s = dscr("ybT_s", [NTT, 128, 256], BF16)
    sza_s = dscr("sza_s", [NTT * 128, 512], BF16)
    szc_s = dscr("szc_s", [NTT * 128, 512], BF16)
    Y_s = dscr("Y_s", [4, NTT * 128, 64], BF16)

    AR_ELEMS = 103 * 1024
    arena_t = es.enter_context(nc.sbuf_tensor("arena", [128, AR_ELEMS], BF16))
    A = Arena(arena_t, AR_ELEMS)
    ident = A.alloc([128, 128], BF16)
    csblk = A.alloc([128, 2, 512], BF16)
    r1 = A.alloc([128, K2], BF16)
    maskP = A.alloc([128, 512], BF16)
    maskN = A.alloc([128, 512], BF16)
    c256 = A.alloc([128, 2, 2, 256], BF16)
    gq = A.alloc([128, 64], F32)
    gk = A.alloc([128, 64], F32)
    expsink = A.alloc([128, 8], F32)
    wsT = A.alloc([128, 4, 128], BF16)
    sgub = A.alloc([128, 4], F32)
    scol = A.alloc([128, 8, 2], F32)
    modsb = A.alloc([128, 24, 2], F32)
    bcol = A.alloc([128, 24], F32)
    KcT = A.alloc([128, CT * 128], BF16)
    Vc = A.alloc([128, CT, 2, 65], BF16)
    fac = A.alloc([128, CT, 512], BF16)
    mhalf = A.alloc([128, 8], F32)
    Vst = [A.alloc([128, 2, 65], BF16) for _ in range(2)]
    P_END = A.off
    WA_OFF = P_END
    wA = A.alloc([128, 8, 2816], BF16)
    T_OFF = A.off
    T_SIZE = 34 * 1024
    WC_OFF = T_OFF + T_SIZE
    A.seek(WC_OFF)
    wg = A.alloc([128, 8, 3072], BF16)
    wpa = A.alloc([128, 2, 1024], BF16)
    wpb = A.alloc([128, 2, 1024], BF16)
    wpc = A.alloc([128, 4, 1024], BF16)
    wout = A.alloc([128, 8, 1024], BF16)
    END_OFF = A.off
    A.seek(WC_OFF)
    X1p = A.alloc([128, 256, 128], BF16)
    assert A.off <= END_OFF

    banks = [es.enter_context(nc.psum_tensor("bank%d" % i, [128, 512], F32)) for i in range(8)]

    def bk(i):
        return banks[i][:]

    def bkb(i):
        return banks[i][:].bitcast(BF16)

    def mm(out, lhsT, rhs, start, stop, reads, writes):
        S.op('pe', lambda e: e.matmul(out, lhsT=lhsT, rhs=rhs, start=start, stop=stop),
             reads=reads, writes=writes)

    def tp(out, in_, reads, writes):
        S.op('pe', lambda e: e.transpose(out=out, in_=in_, identity=ident),
             reads=list(reads) + ['ident'], writes=writes, kind='T')

    def act(out, in_, func, reads, writes, scale=None, bias=None, accum=None):
        kw = {}
        if scale is not None:
            kw['scale'] = scale
        if bias is not None:
            kw['bias'] = bias
        if accum is not None:
            kw['accum_out'] = accum
        S.op('act', lambda e: e.activation(out=out, in_=in_, func=func, **kw),
             reads=reads, writes=writes)

    def tt(eng, out, in0, in1, op, reads, writes):
        S.op(eng, lambda e: e.tensor_tensor(out=out, in0=in0, in1=in1, op=op),
             reads=reads, writes=writes)

    def ts(eng, out, in0, s1, s2, op0, op1, reads, writes):
        if op1 is None:
            S.op(eng, lambda e: e.tensor_scalar(out=out, in0=in0, scalar1=s1, scalar2=None, op0=op0),
                 reads=reads, writes=writes)
        else:
            S.op(eng, lambda e: e.tensor_scalar(out=out, in0=in0, scalar1=s1, scalar2=s2, op0=op0, op1=op1),
                 reads=reads, writes=writes)

    def stt(out, in0, scalar, in1, op0, op1, reads, writes):
        S.op('dve', lambda e: e.scalar_tensor_tensor(out=out, in0=in0, scalar=scalar, in1=in1, op0=op0, op1=op1),
             reads=reads, writes=writes)

    def cp(eng, out, in_, reads, writes):
        if eng == 'act':
            act(out, in_, AF.Copy, reads, writes)
        else:
            S.op(eng, lambda e: e.tensor_copy(out=out, in_=in_), reads=reads, writes=writes)

    def rsqrt_mean(ss, n, tag, width):
        ts('dve', ss, ss, 1.0 / n, EPS, ALU.mult, ALU.add, [tag], [tag])
        tt('pool', ss, ss, mhalf[:, 0:width], ALU.pow, [tag, 'mhalf'], [tag])

    S.dma('sp', 'c0', ident, ident_in[:, :], writes=['ident'])
    S.dma('sp', 'c1', csblk, csblk_in.rearrange("(k p) n -> p k n", p=128), writes=['csblk'])
    S.dma('sp', 'c2', r1[0:K2, :], r1_in[:, :], writes=['r1'])
    S.dma('sp', 'c3', maskP, maskP_in[:, :], writes=['maskP'])
    S.dma('sp', 'c4', maskN, maskN_in[:, :], writes=['maskN'])
    S.dma('sp', 'c5', c256, c256_in[:, :, :, :], writes=['c256'])
    S.dma('sp', 'c6', scol, ccol_in[:, :, :], writes=['scol'])
    S.op('pool', lambda e: e.memset(mhalf, -0.5), writes=['mhalf'])
    for i in range(2):
        S.op('pool', lambda e, i=i: e.memset(Vst[i], 1.0), writes=['Vst%d' % i])
    S.op('pool', lambda e: e.memset(Vc, 1.0), writes=['Vc'])
    act(scol, scol, AF.Silu, ['scol'], ['scol'])

    def layer(l):
        last = (l == NL - 1)
        xsrc = x_in if l == 0 else y
        csrc = ctx_in if l == 0 else xc_s
        S.barrier()
        TA = Arena(arena_t, AR_ELEMS)
        TA.seek(T_OFF)
        S.dma('sp', 'c0', gq, qg_in[l, :].partition_broadcast(128), writes=['gq'])
        S.dma('sp', 'c1', gk, kg_in[l, :].partition_broadcast(128), writes=['gk'])
        S.dma('sp', 'c2', expsink, sink_in[l, :].partition_broadcast(128), writes=['expsink'])
        S.dma('sp', 'c3', sgub, sgu_bT[l, :, :], writes=['sgub'])
        wsT = TA.alloc([128, 4, 128], BF16)
        wsf = TA.alloc([128, 4, 128], F32)
        wsb = TA.alloc([128, 4, 128], BF16)
        for g in range(4):
            S.dma('sp', 'c4', wsf[:, g, :], sgu_w_in[l, g, :, :], writes=['wsf'])
        cp('dve', wsb.rearrange("p a b -> p (a b)"), wsf.rearrange("p a b -> p (a b)"), ['wsf'], ['wsb'])
        for g in range(4):
            tp(bkb(7)[:, g * 128:(g + 1) * 128], wsb[:, g, :], ['wsb'], ['b7'])
        cp('dve', wsT.rearrange("p a b -> p (a b)"), bkb(7)[:, 0:512], ['b7'], ['wsT'])
        S.dma('sp', 'c5', bcol, b_col[l, :, :], writes=['bcol'])
        act(expsink, expsink, AF.Exp, ['expsink'], ['expsink'])
        WA_ = Arena(arena_t, AR_ELEMS)
        WA_.seek(WC_OFF)
        wad = [WA_.alloc([128, 8, 512], F32) for _ in range(2)]
        wft = WA_.alloc([128, 2, 1024], BF16)
        for jc in range(6):
            wb = wad[jc % 2]
            wn = 'wad%d' % (jc % 2)
            S.dma('sp', wn, wb, w_ada[l, :, jc * 512:(jc + 1) * 512].rearrange("(k p) n -> p k n", p=128),
                  writes=[wn])
            for m in range(4):
                col = (jc * 4 + m) * 2
                for k in range(8):
                    mm(bk(0)[:, col:col + 2], wb[:, k, m * 128:(m + 1) * 128], scol[:, k, :],
                       k == 0, k == 7, [wn, 'scol'], ['b0'])
        tt('dve', modsb, bk(0)[:, 0:48].rearrange("p (a b) -> p a b", b=2),
           bcol.unsqueeze(2).to_broadcast([128, 24, 2]), ALU.add, ['b0', 'bcol'], ['modsb'])
        ts('dve', modsb[:, 8:16, :], modsb[:, 8:16, :], 1.0, None, ALU.add, None, ['modsb'], ['modsb'])
        ts('dve', modsb[:, 16:24, :], modsb[:, 16:24, :], 0.5, None, ALU.mult, None, ['modsb'], ['modsb'])
        for r in range(2):
            S.dma('sp', 'gs%d' % r, gate_s[r, :].rearrange("(k p) -> p k", p=128), modsb[:, 16:24, r],
                  reads=['modsb'], writes=['gate_s%d' % r], allow_slow_non_contiguous=True)
        if STOP == 'adaln':
            return True
        for k in range(8):
            S.dma('pool', 'wl%d' % (k % 4), wA[:, k, 512:2816], w_inA[l, k * 128:(k + 1) * 128, :],
                  writes=['wA'])
        for k in range(2):
            S.dma('pool', 'wl%d' % k, wft[:, k, :], w_faT[l, k * 128:(k + 1) * 128, :], writes=['wft'])
        for dk in range(8):
            b = 1 + dk % 2
            for k in range(2):
                mm(bk(b), wft[:, k, dk * 128:(dk + 1) * 128], csblk[:, k, :], k == 0, k == 1,
                   ['wft', 'csblk'], ['b%d' % b])
            cp('act' if dk % 2 else 'dve', wA[:, dk, 0:512], bk(b), ['b%d' % b], ['wA'])

        def twoA(shape, dt):
            return [TA.alloc(shape, dt) for _ in range(2)]
        xt = twoA([128, 1024], F32)
        junk = twoA([128, 1024], BF16)
        nb = twoA([128, 1024], BF16)
        hT = twoA([128, 8, 128], BF16)
        ss = twoA([128, 16], F32)
        faT_sb = twoA([128, 4, 128], BF16)
        qgf = twoA([128, 640], F32)
        sqf = twoA([128, 640], F32)
        t1f = twoA([128, 640], F32)
        t2f = twoA([128, 640], F32)
        qrb = twoA([128, 640], BF16)
        QT_sb = twoA([128, 512], BF16)
        KT_sb = twoA([128, 128], BF16)
        vn = twoA([128, 256], BF16)
        szb = twoA([128, 256], F32)
        uz = twoA([128, 256], F32)
        sza = twoA([128, 512], BF16)
        szc = twoA([128, 512], BF16)
        rC = twoA([128, 64], F32)
        rS = twoA([128, 64], F32)
        assert TA.off <= T_OFF + T_SIZE, ("passA transients", TA.off - T_OFF)

        def passA_chunks(t, isctx, s):
            r = 1 if isctx else 0
            ti = t - NT if isctx else t
            src = csrc if isctx else xsrc
            B = [4 * s + i for i in range(4)]
            bn = ['b%d' % b for b in B]
            sfx = '%d' % s
            R = lambda nm: nm + sfx
            xn, hn = R('xt'), R('hT')
            x_t, h_t, ss_ = xt[s], hT[s], ss[s]
            only_kv = isctx and last

            def c_loads():
                S.dma('sp', xn, x_t, src[ti * 128:(ti + 1) * 128, :], reads=['xres%d' % t], writes=[xn])

            def c_loads2():
                if not isctx:
                    S.dma('sp', R('rC'), rC[s], ropeC_in[ti * 128:(ti + 1) * 128, :], writes=[R('rC')])
                    S.dma('sp', R('rS'), rS[s], ropeS_in[ti * 128:(ti + 1) * 128, :], writes=[R('rS')])

            def c_rms():
                act(junk[s], x_t, AF.Square, [xn], [R('junk'), R('ss0')], accum=ss_[:, 0:1])
                ts('dve', ss_[:, 0:1], ss_[:, 0:1], 1.0 / D, EPS, ALU.mult, ALU.add, [R('ss0')], [R('ss0')])
                tt('pool', ss_[:, 0:1], ss_[:, 0:1], mhalf[:, 0:1], ALU.pow, [R('ss0'), 'mhalf'], [R('ss0')])
                act(nb[s], x_t, AF.Identity, [xn, R('ss0')], [R('nb')], scale=ss_[:, 0:1])

            def c_rms_b():
                tb = bkb(B[0])
                for k in range(8):
                    tp(tb[:, k * 128:(k + 1) * 128], nb[s][:, k * 128:(k + 1) * 128], [R('nb')], [bn[0]])
                for k in range(8):
                    if k % 2 == 0:
                        ts('dve', h_t[:, k, :], tb[:, k * 128:(k + 1) * 128], modsb[:, 8 + k, r:r + 1],
                           modsb[:, k, r:r + 1], ALU.mult, ALU.add, [bn[0], 'modsb'], [hn])
                    else:
                        act(h_t[:, k, :], tb[:, k * 128:(k + 1) * 128], AF.Identity, [bn[0], 'modsb'], [hn],
                            scale=modsb[:, 8 + k, r:r + 1], bias=modsb[:, k, r:r + 1])
                if not only_kv:
                    S.dma('act', R('hTs'), hT_s[t, :, :], h_t.rearrange("p k n -> p (k n)"), reads=[hn],
                          writes=['hT_s%d' % t])

            def c_fa():
                if not isctx:
                    for c4 in range(4):
                        for k in range(8):
                            mm(bk(B[1])[:, c4 * 128:(c4 + 1) * 128], wA[:, k, c4 * 128:(c4 + 1) * 128], h_t[:, k, :],
                               k == 0, k == 7, ['wA', hn], [bn[1]])
                    cp('act', faT_sb[s].rearrange("p a b -> p (a b)"), bk(B[1]), [bn[1]], [R('faT_sb')])
                    S.dma('act', R('faTs'), faT_s.rearrange("(c p) n -> p c n", p=128)[:, :, ti * 128:(ti + 1) * 128],
                          faT_sb[s], reads=[R('faT_sb')], writes=['faT_s%d' % ti])
                else:
                    for k in range(8):
                        mm(bk(B[1]), h_t[:, k, :], wA[:, k, 0:512], k == 0, k == 7, ['wA', hn], [bn[1]])
                    cp('act', fac[:, ti, :], bk(B[1]), [bn[1]], ['fac'])

            def proj(bi, c0, c1):
                for k in range(8):
                    mm(bk(B[bi])[:, 0:c1 - c0], h_t[:, k, :], wA[:, k, c0:c1], k == 0, k == 7, ['wA', hn], [bn[bi]])

            def qk(ps, c0, nh, gain, gname, bname):
                w = nh * 64
                v3 = lambda ap: ap[:, c0:c0 + w].rearrange("p (h d) -> p h d", d=64)
                tt('dve', v3(qgf[s]), ps.rearrange("p (h d) -> p h d", d=64),
                   gain.unsqueeze(1).to_broadcast([128, nh, 64]), ALU.mult, [bname, gname], [R('qgf')])
                act(sqf[s][:, c0:c0 + w], ps, AF.Square, [bname], [R('sqf')])
                sname = R('ssq%d_' % c0)
                sv = ss_[:, 2:2 + nh] if c0 == 0 else ss_[:, 12:12 + nh]
                S.op('dve', lambda e: e.tensor_reduce(out=sv, in_=v3(sqf[s]), axis=AX.X, op=ALU.add),
                     reads=[R('sqf')], writes=[sname])
                ts('dve', sv, sv, 1.0 / 64, EPS, ALU.mult, ALU.add, [sname], [sname])
                tt('pool', sv, sv, mhalf[:, 0:nh], ALU.pow, [sname, 'mhalf'], [sname])
                if not isctx:
                    cb = rC[s].unsqueeze(1).to_broadcast([128, nh, 64])
                    tt('pool', v3(t1f[s]), v3(qgf[s]), cb, ALU.mult, [R('qgf'), R('rC')], [R('t1f')])
                    v5 = lambda ap: ap[:, c0:c0 + w].rearrange("p (h b i d) -> p h b i d", b=2, i=2, d=16)
                    s5 = rS[s].rearrange("p (b i d) -> p b i d", b=2, i=2)
                    for i in range(2):
                        tt('pool', v5(t2f[s])[:, :, :, i, :], v5(qgf[s])[:, :, :, 1 - i, :],
                           s5[:, :, i, :].unsqueeze(1).to_broadcast([128, nh, 2, 16]), ALU.mult,
                           [R('qgf'), R('rS')], [R('t2f')])
                    tt('pool', v3(t1f[s]), v3(t1f[s]), v3(t2f[s]), ALU.add, [R('t1f'), R('t2f')], [R('t1f')])
                    srcq, srcn = t1f[s], R('t1f')
                else:
                    srcq, srcn = qgf[s], R('qgf')
                tt('dve', v3(qrb[s]), v3(srcq), sv.unsqueeze(2).to_broadcast([128, nh, 64]), ALU.mult,
                   [srcn, sname], [R('qrb')])

            def c_qkv_proj():
                if not only_kv:
                    proj(2, 512, 1024)
                proj(3, 1024, 1536)

            def c_qpost():
                qk(bk(B[2]), 0, 8, gq, 'gq', bn[2])

            def c_kvpost():
                qk(bk(B[3])[:, 0:128], 512, 2, gk, 'gk', bn[3])
                vsrc = bk(B[3])[:, 128:256].rearrange("p (h d) -> p h d", d=64)
                if isctx:
                    cp('act', Vc[:, ti, :, 0:64], vsrc, [bn[3]], ['Vc'])
                else:
                    cp('act', Vst[s][:, :, 0:64], vsrc, [bn[3]], [R('Vst')])
                    S.dma('act', R('Vs'), V_s[ti * 128:(ti + 1) * 128, :], Vst[s].rearrange("p a b -> p (a b)"),
                          reads=[R('Vst')], writes=['V_s%d' % ti])
                if not only_kv:
                    act(junk[s][:, 0:256], bk(B[3])[:, 256:512], AF.Square, [bn[3]], [R('junk'), R('ss1')],
                        accum=ss_[:, 1:2])
                    ts('dve', ss_[:, 1:2], ss_[:, 1:2], 1.0 / 256, EPS, ALU.mult, ALU.add, [R('ss1')], [R('ss1')])
                    tt('pool', ss_[:, 1:2], ss_[:, 1:2], mhalf[:, 0:1], ALU.pow, [R('ss1'), 'mhalf'], [R('ss1')])
                    act(vn[s], bk(B[3])[:, 256:512], AF.Identity, [bn[3], R('ss1')], [R('vn')], scale=ss_[:, 1:2])

            def c_uz_proj():
                proj(0, 1536, 2048)
                proj(1, 2048, 2560)

            def c_uzpost():
                act(szb[s], bk(B[0])[:, 256:512], AF.Silu, [bn[0]], [R('szb')])
                tt('dve', uz[s], bk(B[0])[:, 0:256], szb[s], ALU.mult, [bn[0], R('szb')], [R('uz')])
                act(szc[s], bk(B[1]), AF.Silu, [bn[1]], [R('szc')])
                S.dma('act', R('szcs'), szc_s[t * 128:(t + 1) * 128, :], szc[s], reads=[R('szc')],
                      writes=['szc_s%d' % t])

            def c_sgu():
                proj(2, 2560, 2816)
                for g in range(4):
                    mm(bk(B[3])[:, g * 128:(g + 1) * 128], wsT[:, g, :], vn[s][:, (g // 2) * 128:(g // 2) * 128 + 128],
                       True, True, ['wsT', R('vn')], [bn[3]])
                act(sza[s][:, 0:256], bk(B[2])[:, 0:256], AF.Silu, [bn[2]], [R('sza')])
                for g in range(4):
                    stt(sza[s][:, 256 + g * 64:256 + (g + 1) * 64],
                        bk(B[3])[:, g * 128 + (g % 2) * 64:g * 128 + (g % 2) * 64 + 64], sgub[:, g:g + 1],
                        uz[s][:, g * 64:(g + 1) * 64], ALU.add, ALU.mult, [bn[3], R('uz'), 'sgub'], [R('sza')])
                S.dma('act', R('szas'), sza_s[t * 128:(t + 1) * 128, :], sza[s], reads=[R('sza')],
                      writes=['sza_s%d' % t])

            def c_qkT():
                tb = bkb(B[0])
                if not only_kv:
                    for j in range(4):
                        tp(tb[:, j * 128:(j + 1) * 128], qrb[s][:, j * 128:(j + 1) * 128], [R('qrb')], [bn[0]])
                tp(tb[:, 512:640], qrb[s][:, 512:640], [R('qrb')], [bn[0]])
                if not only_kv:
                    cp('act', QT_sb[s], tb[:, 0:512], [bn[0]], [R('QT_sb')])
                    S.dma('act', R('QTs'), QT_s[t, :, :], QT_sb[s], reads=[R('QT_sb')], writes=['QT_s%d' % t])
                if isctx:
                    cp('act', KcT[:, ti * 128:(ti + 1) * 128], tb[:, 512:640], [bn[0]], ['KcT'])
                else:
                    cp('act', KT_sb[s], tb[:, 512:640], [bn[0]], [R('KT_sb')])
                    S.dma('act', R('KTs'), KT_s[:, ti * 128:(ti + 1) * 128], KT_sb[s], reads=[R('KT_sb')],
                          writes=['KT_s%d' % ti])

            def c_x1():
                for a in range(2):
                    S.dma('sp', 'x1l%d' % a, X1p[a * NT + ti:a * NT + ti + 1, :, :],
                          faT_s[a * 256:(a + 1) * 256, ti * 128:(ti + 1) * 128].unsqueeze(0),
                          reads=['faT_s%d' % ti], writes=['X1p', 'wad0', 'wad1', 'wft'])

            head = [(-EARLY_A, c_loads), (-EARLY_A + 0.5, c_rms), (-2.5, c_rms_b), (-2.4, c_loads2)]
            if only_kv:
                return head + [(1.0, c_qkv_proj), (2.0, c_kvpost), (8.0, c_qkT)]
            tlA = [(0.0, c_fa), (1.0, c_qkv_proj), (2.0, c_kvpost), (3.0, c_qpost), (4.0, c_uz_proj),
                   (5.0, c_uzpost), (6.0, c_sgu), (8.0, c_qkT)]
            if not isctx:
                tlA.append((9.0, c_x1))
            return head + tlA

        if STOP == 'dbg_ws':
            S.dma('sp', 'c0', ybT_s[NT, :, :], wsT.rearrange("p a b -> p (a b)")[:, 0:256], reads=['wsT'], writes=['dbgx'])
        if STOP == 'wA':
            return True
        seqA = [(t, True) for t in range(NT, NTT)] + [(t, False) for t in range(NT)]
        EARLY_A = 5.0
        listsA = [passA_chunks(t, ic, i % 2) for i, (t, ic) in enumerate(seqA)]
        HA = 6
        assert max(c[-1][0] for c in listsA) + 1 <= 2 * HA and EARLY_A < HA
        itemsA = sorted((i * HA + c[j][0], i, j) for i, c in enumerate(listsA) for j in range(len(c)))
        for _, i, j in itemsA:
            listsA[i][j][1]()
        if STOP == 'passA':
            return True

        S.barrier()
        FA = Arena(arena_t, AR_ELEMS)
        FA.seek(WA_OFF)
        Gc = [FA.alloc([128, K2, 64], BF16) for _ in range(2)]
        Yc = [FA.alloc([128, NT, 64], BF16) for _ in range(2)]
        KG = min(8, NT)
        Mst = [FA.alloc([128, KG, 2, 128], BF16) for _ in range(2)]
        Yct = FA.alloc([128, 256], BF16)
        assert FA.off <= WC_OFF
        ysc = 1.0 / np.sqrt(float(L))
        if not last:
            for kt in range(2):
                i = 0
                for nt in range(2):
                    for a in range(2):
                        mm(bk(7)[:, 0:256], c256[:, nt, a, kt * 128:(kt + 1) * 128], fac[:, nt, a * 256:(a + 1) * 256],
                           i == 0, i == 3, ['c256', 'fac'], ['b7'])
                        i += 1
                act(Yct, bk(7)[:, 0:256], AF.Copy, ['b7'], ['Yct'], scale=1.0 / 16.0)
                S.dma('sp', 'Ycts', Y_s[:, L + kt * 128:L + (kt + 1) * 128, :].rearrange("c n e -> n c e"),
                      Yct.rearrange("p (c e) -> p c e", c=4), reads=['Yct'], writes=['Y_sc%d' % kt])
        def load_passC_weights():
            dead = ['X1p', 'wad0', 'wad1', 'wft']
            for k in range(8):
                S.dma('pool', 'wl%d' % (k % 4), wg[:, k, :], w_g[l, k * 128:(k + 1) * 128, :], writes=['wg'] + dead)
            for k in range(2):
                S.dma('pool', 'wl%d' % k, wpa[:, k, :], w_pa[l, k * 128:(k + 1) * 128, :], writes=['wpa'] + dead)
                S.dma('pool', 'wl%d' % (2 + k), wpb[:, k, :], w_pb[l, k * 128:(k + 1) * 128, :], writes=['wpb'] + dead)
            for k in range(4):
                S.dma('pool', 'wl%d' % k, wpc[:, k, :], w_pc[l, k * 128:(k + 1) * 128, :], writes=['wpc'] + dead)
            for k in range(8):
                S.dma('pool', 'wl%d' % (k % 4), wout[:, k, :], w_out[l, k * 128:(k + 1) * 128, :],
                      writes=['wout'] + dead)

        mcnt = 0
        for c in range(4):
            g_t = Gc[c % 2]
            gn = 'Gc%d' % (c % 2)
            y_t = Yc[c % 2]
            yn = 'Yc%d' % (c % 2)
            for q4 in range(16):
                b = q4 % 2
                for i in range(4):
                    ch = c * 64 + q4 * 4 + i
                    mm(bk(b)[:, i * K2:(i + 1) * K2], X1p[0:K2, ch, :], r1[0:K2, :], True, True,
                       ['X1p', 'r1'], ['b%d' % b])
                cp('act' if q4 % 2 else 'dve',
                   g_t[:, :, q4 * 4:q4 * 4 + 4].rearrange("p j c -> p c j"),
                   bk(b)[:, 0:4 * K2].rearrange("p (c j) -> p c j", c=4), ['b%d' % b], [gn])
            if c == 3:
                load_passC_weights()
            for kg in range(NT // KG):
                m_t = Mst[mcnt % 2]
                mn = 'Mst%d' % (mcnt % 2)
                mcnt += 1
                S.dma('sp', mn, m_t, m2_in[:, kg * KG:(kg + 1) * KG, :, :], writes=[mn])
                b = 2 + kg % 2
                for kk in range(KG):
                    k2 = kg * KG + kk
                    for r_ in range(2):
                        mm(bk(b)[:, kk * 64:(kk + 1) * 64], m_t[:, kk, r_, :], g_t[:, 2 * k2 + r_, :],
                           r_ == 0, r_ == 1, [mn, gn], ['b%d' % b])
                act(y_t[:, kg * KG:(kg + 1) * KG, :].rearrange("p a b -> p (a b)"), bk(b)[:, 0:KG * 64], AF.Copy,
                    ['b%d' % b], [yn], scale=float(ysc))
            S.dma('sp', yn + 's', Y_s[c, 0:L, :].rearrange("(k1 k2) e -> k1 k2 e", k2=NT), y_t,
                  reads=[yn], writes=['Y_s'])

        if STOP == 'fourier':
            return True
        S.barrier()
        CA = Arena(arena_t, AR_ELEMS)
        CA.seek(WA_OFF)
        gate_bc = [CA.alloc([128, 1024], F32) for _ in range(2)]
        for r in range(2):
            S.dma('sp', 'gb%d' % r, gate_bc[r], gate_s[r, :].partition_broadcast(128), reads=['gate_s%d' % r],
                  writes=['gate_bc%d' % r])
        def two(shape, dt):
            return [CA.alloc(shape, dt) for _ in range(2)]
        xt = two([128, 1024], F32)
        hT = two([128, 8, 128], BF16)
        QTt = two([128, 512], BF16)
        KTw = two([128, 3, 128], BF16)
        Vw = two([128, 3, 130], BF16)
        szc = two([128, 512], BF16)
        sza = two([128, 512], BF16)
        Yt = two([128, 256], BF16)
        E = two([128, 5, 2, 512], BF16)
        tg = two([128, 3072], BF16)
        den = two([128, 8], F32)
        atf = two([128, 512], F32)
        ycb = two([128, 512], BF16)
        yab = two([128, 256], BF16)
        yT = two([128, 8, 128], BF16)
        m1 = two([128, 512], F32)
        m2 = two([128, 512], F32)
        m3 = two([128, 512], F32)
        mg = two([128, 1024], BF16)
        mT = two([128, 8, 128], BF16)
        xo = two([128, 1024], F32)
        assert CA.off <= WC_OFF, ("passC transients", CA.off, WC_OFF)

        def passC_chunks(t, isctx, s):
            r = 1 if isctx else 0
            ti = t - NT if isctx else t
            src = csrc if isctx else xsrc
            dst = xc_s if isctx else y
            B = [4 * s + i for i in range(4)]
            bn = ['b%d' % b for b in B]
            sfx = '%d' % s
            xn, hn = 'xt' + sfx, 'hT' + sfx
            R = lambda nm: nm + sfx
            kbs = []
            if not isctx:
                lo = max(ti - 1, 0)
                hi = min(ti + 1, NT - 1)
                nb_ = hi - lo + 1
                for u in range(lo, hi + 1):
                    mk = None if u == ti else (maskP if u < ti else maskN)
                    kbs.append((KTw[s][:, u - lo, :], Vw[s][:, u - lo, :].rearrange("p (h e) -> p h e", h=2), mk,
                                [R('KTw')], [R('Vw')]))
            for u in range(CT):
                kbs.append((KcT[:, u * 128:(u + 1) * 128], Vc[:, u, :, :], None, ['KcT'], ['Vc']))
            nk = len(kbs)

            def c_loads_early():
                S.dma('sp', hn, hT[s].rearrange("p k n -> p (k n)"), hT_s[t, :, :], reads=['hT_s%d' % t], writes=[hn])
                S.dma('sp', R('QT'), QTt[s], QT_s[t, :, :], reads=['QT_s%d' % t], writes=[R('QT')])
                if not isctx:
                    S.dma('sp', R('KTw'), KTw[s][:, 0:nb_, :].rearrange("p a b -> p (a b)"),
                          KT_s[:, lo * 128:(hi + 1) * 128], reads=['KT_s%d' % u for u in range(lo, hi + 1)],
                          writes=[R('KTw')])

            def c_loads():
                S.dma('sp', xn, xt[s], src[ti * 128:(ti + 1) * 128, :], reads=['xres%d' % t], writes=[xn])
                S.dma('sp', R('szc'), szc[s], szc_s[t * 128:(t + 1) * 128, :], reads=['szc_s%d' % t],
                      writes=[R('szc')])
                S.dma('sp', R('sza'), sza[s], sza_s[t * 128:(t + 1) * 128, :], reads=['sza_s%d' % t],
                      writes=[R('sza')])
                S.dma('sp', R('Yt'), Yt[s].rearrange("p (c e) -> p c e", c=4),
                      Y_s[:, t * 128:(t + 1) * 128, :].rearrange("c n e -> n c e"),
                      reads=['Y_s', 'Y_sc0', 'Y_sc1'], writes=[R('Yt')])
                if not isctx:
                    S.dma('sp', R('Vw'), Vw[s][:, 0:nb_, :],
                          V_s[lo * 128:(hi + 1) * 128, :].rearrange("(a p) e -> p a e", p=128),
                          reads=['V_s%d' % u for u in range(lo, hi + 1)], writes=[R('Vw')])

            def c_score(bi):
                kT, vv, mk, kr, vr = kbs[bi]
                for kv in range(2):
                    b = B[kv]
                    mm(bk(b), kT[kv * 64:(kv + 1) * 64, :], QTt[s][kv * 64:(kv + 1) * 64, :], True, mk is None,
                       kr + [R('QT')], [bn[kv]])
                    if mk is not None:
                        mm(bk(b), ident, mk, False, True, ['ident', 'maskP', 'maskN'], [bn[kv]])
                for kv in range(2):
                    act(E[s][:, bi, kv, :], bk(B[kv]), AF.Exp, [bn[kv]], [R('E')], scale=0.125)

            def c_gate(c):
                b = B[2 + c % 2]
                for k in range(8):
                    mm(bk(b), hT[s][:, k, :], wg[:, k, c * 512:(c + 1) * 512], k == 0, k == 7, ['wg', hn],
                       [bn[2 + c % 2]])
                act(tg[s][:, c * 512:(c + 1) * 512], bk(b), AF.Tanh, [bn[2 + c % 2]], [R('tg')], scale=0.5)

            def c_pv(kv):
                ob = bk(B[2 + kv])[:, 0:260].rearrange("p (j e) -> p j e", e=65)
                for j in range(4):
                    for bi, (kT, vv, mk, kr, vr) in enumerate(kbs):
                        mm(ob[:, j, :], E[s][:, bi, kv, j * 128:(j + 1) * 128], vv[:, kv, :], bi == 0,
                           bi == nk - 1, [R('E')] + vr, [bn[2 + kv]])
                tt('dve', den[s][:, kv * 4:(kv + 1) * 4], ob[:, :, 64], expsink[:, kv * 4:(kv + 1) * 4], ALU.add,
                   [bn[2 + kv], 'expsink'], [R('den')])

            def c_norm():
                S.op('dve', lambda e: e.reciprocal(out=den[s], in_=den[s]), reads=[R('den')], writes=[R('den')])
                for kv in range(2):
                    ob = bk(B[2 + kv])[:, 0:260].rearrange("p (j e) -> p j e", e=65)
                    tt('dve', atf[s][:, kv * 256:(kv + 1) * 256].rearrange("p (j d) -> p j d", d=64), ob[:, :, 0:64],
                       den[s][:, kv * 4:(kv + 1) * 4].unsqueeze(2).to_broadcast([128, 4, 64]), ALU.mult,
                       [bn[2 + kv], R('den')], [R('atf')])
                tt('pool', ycb[s], atf[s], szc[s], ALU.mult, [R('atf'), R('szc')], [R('ycb')])
                tt('pool', yab[s], Yt[s], sza[s][:, 0:256], ALU.mult, [R('Yt'), R('sza')], [R('yab')])

            def c_tr():
                tb = bkb(B[0])
                for j in range(2):
                    tp(tb[:, j * 128:(j + 1) * 128], yab[s][:, j * 128:(j + 1) * 128], [R('yab')], [bn[0]])
                for j in range(4):
                    tp(tb[:, (2 + j) * 128:(3 + j) * 128], ycb[s][:, j * 128:(j + 1) * 128], [R('ycb')], [bn[0]])
                for j in range(2):
                    tp(tb[:, (6 + j) * 128:(7 + j) * 128], sza[s][:, 256 + j * 128:256 + (j + 1) * 128],
                       [R('sza')], [bn[0]])
                cp('act', yT[s].rearrange("p a b -> p (a b)"), tb, [bn[0]], [R('yT')])

            def c_proj(hD):
                cs = slice(hD * 512, (hD + 1) * 512)
                for k in range(2):
                    mm(bk(B[1]), yT[s][:, k, :], wpa[:, k, cs], k == 0, k == 1, [R('yT'), 'wpa'], [bn[1]])
                for k in range(2):
                    mm(bk(B[2]), yT[s][:, 6 + k, :], wpb[:, k, cs], k == 0, k == 1, [R('yT'), 'wpb'], [bn[2]])
                for k in range(4):
                    mm(bk(B[3]), yT[s][:, 2 + k, :], wpc[:, k, cs], k == 0, k == 3, [R('yT'), 'wpc'], [bn[3]])
                stt(m1[s], tg[s][:, hD * 512:(hD + 1) * 512], 1.0, bk(B[1]), ALU.add, ALU.mult, [R('tg'), bn[1]],
                    [R('m1')])
                stt(m2[s], tg[s][:, 1024 + hD * 512:1024 + (hD + 1) * 512], 1.0, bk(B[2]), ALU.add, ALU.mult,
                    [R('tg'), bn[2]], [R('m2')])
                stt(m3[s], tg[s][:, 2048 + hD * 512:2048 + (hD + 1) * 512], 1.0, bk(B[3]), ALU.add, ALU.mult,
                    [R('tg'), bn[3]], [R('m3')])
                tt('pool', m1[s], m1[s], m2[s], ALU.add, [R('m1'), R('m2')], [R('m1')])
                tt('pool', mg[s][:, cs], m1[s], m3[s], ALU.add, [R('m1'), R('m3')], [R('mg')])

            def c_mt():
                tb = bkb(B[0])
                for k in range(8):
                    tp(tb[:, k * 128:(k + 1) * 128], mg[s][:, k * 128:(k + 1) * 128], [R('mg')], [bn[0]])
                cp('act', mT[s].rearrange("p a b -> p (a b)"), tb, [bn[0]], [R('mT')])

            def c_out(hD):
                cs = slice(hD * 512, (hD + 1) * 512)
                b = B[1 + hD]
                for k in range(8):
                    mm(bk(b), mT[s][:, k, :], wout[:, k, cs], k == 0, k == 7, [R('mT'), 'wout'], [bn[1 + hD]])
                tt('dve', xo[s][:, cs], bk(b), gate_bc[r][:, cs], ALU.mult, [bn[1 + hD], 'gate_bc%d' % r],
                   [R('xo')])

            def c_fin():
                tt('pool', xo[s], xo[s], xt[s], ALU.add, [R('xo'), xn], [R('xo')])
                S.dma('pool', R('xo'), dst[ti * 128:(ti + 1) * 128, :], xo[s], reads=[R('xo')],
                      writes=['xres%d' % t])

            ch = [c_loads]
            sc = [lambda bi=bi: c_score(bi) for bi in range(nk)]
            gg = [lambda c=c: c_gate(c) for c in range(6)]
            while sc or gg:
                if sc:
                    ch.append(sc.pop(0))
                if gg:
                    ch.append(gg.pop(0))
            n_front = len(ch)
            tl = [(float(j), f) for j, f in enumerate(ch)]
            j = float(n_front)
            for f, slack in [(lambda: c_pv(0), 0), (lambda: c_pv(1), 0), (c_norm, 0), (c_tr, 2),
                             (lambda: c_proj(0), 0), (lambda: c_proj(1), 0), (c_mt, 2),
                             (lambda: c_out(0), 0), (lambda: c_out(1), 0), (c_fin, 0)]:
                j += slack
                tl.append((j, f))
                j += 1
            tl.insert(0, (MAXFRONT_C - 2 * HC + 0.5, c_loads_early))
            return tl

        seqC = ([(t, True) for t in range(NT, NTT)] if not last else []) + [(t, False) for t in range(NT)]
        MAXFRONT_C = 11.0
        HC = 13
        lists = [passC_chunks(t, ic, i % 2) for i, (t, ic) in enumerate(seqC)]
        assert max(c[-1][0] for c in lists) + 1 <= 2 * HC
        items = sorted((i * HC + c[j][0], i, j) for i, c in enumerate(lists) for j in range(len(c)))
        for _, i, j in items:
            lists[i][j][1]()

    for l in range(NL):
        if layer(l):
            break
    S.finish('sp')
    es.close()
    return nc


def _bf(a):
    return np.ascontiguousarray(a.astype(ml_dtypes.bfloat16))


def _tables(NT):
    L = NT * 128
    K2 = 2 * NT
    t = {}
    t['ident'] = _bf(np.eye(128, dtype=np.float32))
    j = np.arange(64)
    ang = 2 * np.pi * np.outer(j, j) / 64.0
    cs = np.zeros((256, 512), np.float64)
    for g in range(4):
        cs[g * 64:(g + 1) * 64, g * 64:(g + 1) * 64] = np.cos(ang) / 8.0
        cs[g * 64:(g + 1) * 64, 256 + g * 64:256 + (g + 1) * 64] = np.sin(ang) / 8.0
    t['csblk'] = _bf(cs)
    n2 = np.arange(NT)
    th = 2 * np.pi * np.outer(n2, n2) / NT
    r1 = np.zeros((2, NT, NT, 2), np.float64)
    r1[0, :, :, 0] = np.cos(th)
    r1[1, :, :, 0] = -np.sin(th)
    r1[0, :, :, 1] = np.sin(th)
    r1[1, :, :, 1] = np.cos(th)
    t['r1'] = _bf(r1.reshape(K2, K2))
    n1 = np.arange(128)[:, None, None]
    k2 = np.arange(NT)[None, :, None]
    k1 = np.arange(128)[None, None, :]
    kk = (NT * k1 + k2).astype(np.int64)
    ph = 2 * np.pi * ((n1 * kk) % L) / L
    m2 = np.stack([np.cos(ph), -np.sin(ph)], axis=2)
    t['m2'] = _bf(m2)
    n = (np.arange(2)[None, :, None] * 128 + np.arange(128)[:, None, None])
    k = np.arange(256)[None, None, :]
    ph = 2 * np.pi * ((n * k) % 256) / 256.0
    t['c256'] = _bf(np.stack([np.cos(ph), -np.sin(ph)], axis=2))
    pos = np.arange(L)
    row = (pos // 64).astype(np.float32)
    col = (pos % 64).astype(np.float32)
    inv = (10000.0 ** (-np.arange(16, dtype=np.float32) / 16)).astype(np.float32)
    ar = row[:, None] * inv[None, :]
    ac = col[:, None] * inv[None, :]
    t['ropeC'] = np.concatenate([np.cos(ar), np.cos(ar), np.cos(ac), np.cos(ac)], axis=1).astype(np.float32)
    t['ropeS'] = np.concatenate([-np.sin(ar), np.sin(ar), -np.sin(ac), np.sin(ac)], axis=1).astype(np.float32)
    p = np.arange(128)[:, None]
    f = (np.arange(512) % 128)[None, :]
    t['maskP'] = _bf(np.where(p >= f, 0.0, NEG))
    t['maskN'] = _bf(np.where(p <= f, 0.0, NEG))
    return t


def _prep_shared(NL, w_ada, b_ada, w_in, sgu_w, sgu_b, q_norm_g, k_norm_g, attn_sink, w_pa, w_pb, w_pc, w_out):
    f = lambda a: np.ascontiguousarray(np.asarray(a, dtype=np.float32))
    w_in = np.asarray(w_in, dtype=np.float32)
    qperm = []
    for j in range(4):
        for kv in range(2):
            h = kv * 4 + j
            qperm.extend(range(1280 + h * 64, 1280 + (h + 1) * 64))
    colsA = (qperm + list(range(1792, 1920)) + list(range(1920, 2048)) + list(range(768, 1024))
             + list(range(512, 768)) + list(range(1024, 1280)) + list(range(2048, 2560)) + list(range(256, 512)))
    sh = {}
    sh['w_ada'] = f(w_ada[:NL])
    b = np.asarray(b_ada, dtype=np.float32)[:NL]
    sh['b_col'] = f(b.reshape(NL, 24, 128).transpose(0, 2, 1))
    sh['w_inA'] = f(w_in[:NL][:, :, colsA])
    sh['w_faT'] = f(w_in[:NL][:, :, 0:256].transpose(0, 2, 1))
    sh['w_g'] = f(w_in[:NL][:, :, 2560:5632])
    sh['w_pa'] = f(w_pa[:NL])
    sh['w_pb'] = f(w_pb[:NL])
    sh['w_pc'] = f(w_pc[:NL])
    sh['w_out'] = f(w_out[:NL])
    sh['sgu_w'] = f(sgu_w[:NL])
    sh['sgu_bT'] = f(np.asarray(sgu_b, dtype=np.float32)[:NL].transpose(0, 2, 1))
    sh['qg'] = f(q_norm_g[:NL])
    sh['kg'] = f(k_norm_g[:NL])
    sh['sink'] = f(attn_sink[:NL])
    return sh


_CACHE = {}


def run(x, c, ctx, c_ctx, NL, **params):
    x = np.asarray(x, dtype=np.float32)
    B, L, _ = x.shape
    NT = L // 128
    key = (NT, NL)
    if key not in _CACHE:
        _CACHE[key] = build(NT, NL)
    nc = _CACHE[key]
    shared = _prep_shared(NL, **params)
    shared.update(_tables(NT))
    c = np.asarray(c, dtype=np.float32)
    c_ctx = np.asarray(c_ctx, dtype=np.float32)
    ctx = np.asarray(ctx, dtype=np.float32)
    in_maps = []
    for core in range(8):
        b = core % B
        m = dict(shared)
        m['x'] = np.ascontiguousarray(x[b])
        m['ctx'] = np.ascontiguousarray(ctx[b])
        cc = np.stack([c[b].reshape(8, 128).T, c_ctx.reshape(8, 128).T], axis=2)
        m['ccol'] = np.ascontiguousarray(cc.astype(np.float32))
        in_maps.append(m)
    res = run_bass_kernel_spmd(nc, in_maps, core_ids=list(range(8)))
    if DEBUG:
        return res.results
    return np.stack([res.results[b]["y"] for b in range(B)], axis=0).astype(np.float32)


def kernel(x, c, ctx, c_ctx, w_ada, b_ada, w_in, sgu_w, sgu_b, q_norm_g, k_norm_g,
           attn_sink, w_pa, w_pb, w_pc, w_out):
    return run(x, c, ctx, c_ctx, 4, w_ada=w_ada, b_ada=b_ada, w_in=w_in, sgu_w=sgu_w, sgu_b=sgu_b,
               q_norm_g=q_norm_g, k_norm_g=k_norm_g, attn_sink=attn_sink, w_pa=w_pa, w_pb=w_pb,
               w_pc=w_pc, w_out=w_out)
```

```python
import contextlib
import numpy as np
import ml_dtypes
import concourse.bass as bass
import concourse.mybir as mybir
from concourse.bass_utils import run_bass_kernel_spmd

F32 = mybir.dt.float32
BF16 = mybir.dt.bfloat16
AF = mybir.ActivationFunctionType
ALU = mybir.AluOpType
AX = mybir.AxisListType

DEBUG = False
STOP = None
D = 1024
CT = 2
EPS = 1e-6
NEG = -30000.0


class Sched:
    def __init__(self, nc, es):
        self.nc = nc
        self.es = es
        self.eng = {'pe': nc.tensor, 'act': nc.scalar, 'dve': nc.vector,
                    'pool': nc.gpsimd, 'sp': nc.sync}
        self.sem = {}
        self.cnt = {}
        for k in self.eng:
            self.sem[k] = es.enter_context(nc.semaphore("s_" + k))
            self.cnt[k] = 0
        self.lastw = {}
        self.readers = {}
        self.seen = {}
        self.nchan = 0
        self.chans = {}
        self.pe_kind = None

    def chan(self, name):
        if name in self.chans:
            return self.chans[name]
        k = "ch%d" % self.nchan
        self.nchan += 1
        self.sem[k] = self.es.enter_context(self.nc.semaphore("d_" + k))
        self.cnt[k] = 0
        self.chans[name] = k
        return k

    def _need(self, reads, writes):
        need = {}

        def add(x):
            if x is None:
                return
            if need.get(x[0], 0) < x[1]:
                need[x[0]] = x[1]
        for r in reads:
            add(self.lastw.get(r))
        for w in writes:
            add(self.lastw.get(w))
            for rd in self.readers.get(w, ()):
                add(rd)
        return need

    def _waits(self, e, need):
        for k, c in need.items():
            if k == e and e == 'pe':
                continue
            if self.seen.get((e, k), 0) >= c:
                continue
            self.eng[e].wait_ge(self.sem[k], c)
            self.seen[(e, k)] = c

    def _commit(self, key, reads, writes):
        tok = (key, self.cnt[key])
        for r in reads:
            self.readers.setdefault(r, []).append(tok)
        for w in writes:
            self.lastw[w] = tok
            self.readers[w] = []

    def op(self, e, fn, reads=(), writes=(), kind='M'):
        pb = [r for r in reads if len(r) == 2 and r[0] == 'b' and r[1].isdigit()]
        if pb:
            writes = list(writes) + [r for r in pb if r not in writes]
        self._waits(e, self._need(reads, writes))
        if e == 'pe':
            if self.pe_kind is not None and kind != self.pe_kind and self.cnt['pe'] > 0:
                if self.seen.get(('pe', 'pe'), 0) < self.cnt['pe']:
                    self.eng['pe'].wait_ge(self.sem['pe'], self.cnt['pe'])
                    self.seen[('pe', 'pe')] = self.cnt['pe']
            self.pe_kind = kind
        ins = fn(self.eng[e])
        self.cnt[e] += 1
        ins.then_inc(self.sem[e], 1)
        self._commit(e, reads, writes)

    def dma(self, e, chname, out, in_, reads=(), writes=(), **kw):
        ch = self.chan(chname)
        need = self._need(reads, writes)
        if self.cnt[ch] > 0:
            need[ch] = max(need.get(ch, 0), self.cnt[ch])
        self._waits(e, need)
        ins = self.eng[e].dma_start(out=out, in_=in_, **kw)
        self.cnt[ch] += 16
        ins.then_inc(self.sem[ch], 16)
        self._commit(ch, reads, writes)

    def barrier(self, skip=()):
        sk = set(self.chans[n] for n in skip if n in self.chans)
        for e in self.eng:
            need = {k: c for k, c in self.cnt.items() if c > 0 and k != e and k not in sk}
            self._waits(e, need)

    def finish(self, e='sp'):
        need = {k: c for k, c in self.cnt.items() if c > 0 and k != e}
        self._waits(e, need)


class Arena:
    def __init__(self, tensor, nelem):
        self.t = tensor
        self.n = nelem
        self.off = 0

    def seek(self, off):
        self.off = off

    def alloc(self, shape, dt):
        n = 1
        for s in shape[1:]:
            n *= s
        nb = n * (4 if dt == F32 else 2)
        nb = (nb + 255) // 256 * 256
        ne = nb // 2
        assert self.off + ne <= self.n, ("arena overflow", self.off, ne, self.n)
        ap = self.t[:, self.off:self.off + n * (2 if dt == F32 else 1)]
        self.off += ne
        if dt == F32:
            ap = ap.bitcast(F32)
        if len(shape) == 3:
            ap = ap.rearrange("p (a b) -> p a b", a=shape[1])
        elif len(shape) == 4:
            ap = ap.rearrange("p (a b c) -> p a b c", a=shape[1], b=shape[2])
        return ap


def build(NT, NL):
    L = NT * 128
    NTT = NT + CT
    K2 = 2 * NT
    nc = bass.Bass("TRN2", target_bir_lowering=False)
    es = contextlib.ExitStack()
    S = Sched(nc, es)

    def din(name, shape, dt=F32):
        return nc.dram_tensor(name, shape, dt, kind="ExternalInput").ap()

    def dscr(name, shape, dt):
        return nc.dram_tensor(name, shape, dt, kind="ExternalOutput" if DEBUG else "Internal").ap()

    x_in = din("x", [L, D])
    ctx_in = din("ctx", [CT * 128, D])
    ccol_in = din("ccol", [128, 8, 2])
    w_ada = din("w_ada", [NL, D, 3 * D])
    b_col = din("b_col", [NL, 128, 24])
    w_inA = din("w_inA", [NL, D, 2304])
    w_faT = din("w_faT", [NL, 256, D])
    w_g = din("w_g", [NL, D, 3 * D])
    w_pa = din("w_pa", [NL, 256, D])
    w_pb = din("w_pb", [NL, 256, D])
    w_pc = din("w_pc", [NL, 512, D])
    w_out = din("w_out", [NL, D, D])
    sgu_w_in = din("sgu_w", [NL, 4, 128, 128])
    sgu_bT = din("sgu_bT", [NL, 128, 4])
    qg_in = din("qg", [NL, 64])
    kg_in = din("kg", [NL, 64])
    sink_in = din("sink", [NL, 8])
    ident_in = din("ident", [128, 128], BF16)
    csblk_in = din("csblk", [256, 512], BF16)
    r1_in = din("r1", [K2, K2], BF16)
    m2_in = din("m2", [128, NT, 2, 128], BF16)
    c256_in = din("c256", [128, 2, 2, 256], BF16)
    ropeC_in = din("ropeC", [L, 64])
    ropeS_in = din("ropeS", [L, 64])
    maskP_in = din("maskP", [128, 512], BF16)
    maskN_in = din("maskN", [128, 512], BF16)
    y = nc.dram_tensor("y", [L, D], F32, kind="ExternalOutput").ap()

    xc_s = dscr("xc_s", [CT * 128, D], F32)
    gate_s = dscr("gate_s", [2, D], F32)
    faT_s = dscr("faT_s", [512, L], BF16)
    hT_s = dscr("hT_s", [NTT, 128, D], BF16)
    QT_s = dscr("QT_s", [NTT, 128, 512], BF16)
    KT_s = dscr("KT_s", [128, L], BF16)
    V_s = dscr("V_s", [L, 130], BF16)
    ybT_s = dscr("ybT_s", [NTT, 128, 256], BF16)
    sza_s = dscr("sza_s", [NTT * 128, 512], BF16)
    szc_s = dscr("szc_s", [NTT * 128, 512], BF16)
    Y_s = dscr("Y_s", [4, NTT * 128, 64], BF16)

    AR_ELEMS = 103 * 1024
    arena_t = es.enter_context(nc.sbuf_tensor("arena", [128, AR_ELEMS], BF16))
    A = Arena(arena_t, AR_ELEMS)
    ident = A.alloc([128, 128], BF16)
    csblk = A.alloc([128, 2, 512], BF16)
    r1 = A.alloc([128, K2], BF16)
    maskP = A.alloc([128, 512], BF16)
    maskN = A.alloc([128, 512], BF16)
    c256 = A.alloc([128, 2, 2, 256], BF16)
    gq = A.alloc([128, 64], F32)
    gk = A.alloc([128, 64], F32)
    expsink = A.alloc([128, 8], F32)
    wsT = A.alloc([128, 4, 128], BF16)
    sgub = A.alloc([128, 4], F32)
    scol = A.alloc([128, 8, 2], F32)
    modsb = A.alloc([128, 24, 2], F32)
    bcol = A.alloc([128, 24], F32)
    KcT = A.alloc([128, CT * 128], BF16)
    Vc = A.alloc([128, CT, 2, 65], BF16)
    fac = A.alloc([128, CT, 512], BF16)
    mhalf = A.alloc([128, 8], F32)
    Vst = [A.alloc([128, 2, 65], BF16) for _ in range(2)]
    P_END = A.off
    WA_OFF = P_END
    wA = A.alloc([128, 8, 2816], BF16)
    T_OFF = A.off
    T_SIZE = 34 * 1024
    WC_OFF = T_OFF + T_SIZE
    A.seek(WC_OFF)
    wg = A.alloc([128, 8, 3072], BF16)
    wpa = A.alloc([128, 2, 1024], BF16)
    wpb = A.alloc([128, 2, 1024], BF16)
    wpc = A.alloc([128, 4, 1024], BF16)
    wout = A.alloc([128, 8, 1024], BF16)
    END_OFF = A.off
    A.seek(WC_OFF)
    X1p = A.alloc([128, 256, 128], BF16)
    assert A.off <= END_OFF

    banks = [es.enter_context(nc.psum_tensor("bank%d" % i, [128, 512], F32)) for i in range(8)]

    def bk(i):
        return banks[i][:]

    def bkb(i):
        return banks[i][:].bitcast(BF16)

    def mm(out, lhsT, rhs, start, stop, reads, writes):
        S.op('pe', lambda e: e.matmul(out, lhsT=lhsT, rhs=rhs, start=start, stop=stop),
             reads=reads, writes=writes)

    def tp(out, in_, reads, writes):
        S.op('pe', lambda e: e.transpose(out=out, in_=in_, identity=ident),
             reads=list(reads) + ['ident'], writes=writes, kind='T')

    def act(out, in_, func, reads, writes, scale=None, bias=None, accum=None):
        kw = {}
        if scale is not None:
            kw['scale'] = scale
        if bias is not None:
            kw['bias'] = bias
        if accum is not None:
            kw['accum_out'] = accum
        S.op('act', lambda e: e.activation(out=out, in_=in_, func=func, **kw),
             reads=reads, writes=writes)

    def tt(eng, out, in0, in1, op, reads, writes):
        S.op(eng, lambda e: e.tensor_tensor(out=out, in0=in0, in1=in1, op=op),
             reads=reads, writes=writes)

    def ts(eng, out, in0, s1, s2, op0, op1, reads, writes):
        if op1 is None:
            S.op(eng, lambda e: e.tensor_scalar(out=out, in0=in0, scalar1=s1, scalar2=None, op0=op0),
                 reads=reads, writes=writes)
        else:
            S.op(eng, lambda e: e.tensor_scalar(out=out, in0=in0, scalar1=s1, scalar2=s2, op0=op0, op1=op1),
                 reads=reads, writes=writes)

    def stt(out, in0, scalar, in1, op0, op1, reads, writes):
        S.op('dve', lambda e: e.scalar_tensor_tensor(out=out, in0=in0, scalar=scalar, in1=in1, op0=op0, op1=op1),
             reads=reads, writes=writes)

    def cp(eng, out, in_, reads, writes):
        if eng == 'act':
            act(out, in_, AF.Copy, reads, writes)
        else:
            S.op(eng, lambda e: e.tensor_copy(out=out, in_=in_), reads=reads, writes=writes)

    def rsqrt_mean(ss, n, tag, width):
        ts('dve', ss, ss, 1.0 / n, EPS, ALU.mult, ALU.add, [tag], [tag])
        tt('pool', ss, ss, mhalf[:, 0:width], ALU.pow, [tag, 'mhalf'], [tag])

    S.dma('sp', 'c0', ident, ident_in[:, :], writes=['ident'])
    S.dma('sp', 'c1', csblk, csblk_in.rearrange("(k p) n -> p k n", p=128), writes=['csblk'])
    S.dma('sp', 'c2', r1[0:K2, :], r1_in[:, :], writes=['r1'])
    S.dma('sp', 'c3', maskP, maskP_in[:, :], writes=['maskP'])
    S.dma('sp', 'c4', maskN, maskN_in[:, :], writes=['maskN'])
    S.dma('sp', 'c5', c256, c256_in[:, :, :, :], writes=['c256'])
    S.dma('sp', 'c6', scol, ccol_in[:, :, :], writes=['scol'])
    S.op('pool', lambda e: e.memset(mhalf, -0.5), writes=['mhalf'])
    for i in range(2):
        S.op('pool', lambda e, i=i: e.memset(Vst[i], 1.0), writes=['Vst%d' % i])
    S.op('pool', lambda e: e.memset(Vc, 1.0), writes=['Vc'])
    act(scol, scol, AF.Silu, ['scol'], ['scol'])

    def layer(l):
        last = (l == NL - 1)
        xsrc = x_in if l == 0 else y
        csrc = ctx_in if l == 0 else xc_s
        S.barrier()
        TA = Arena(arena_t, AR_ELEMS)
        TA.seek(T_OFF)
        S.dma('sp', 'c0', gq, qg_in[l, :].partition_broadcast(128), writes=['gq'])
        S.dma('sp', 'c1', gk, kg_in[l, :].partition_broadcast(128), writes=['gk'])
        S.dma('sp', 'c2', expsink, sink_in[l, :].partition_broadcast(128), writes=['expsink'])
        S.dma('sp', 'c3', sgub, sgu_bT[l, :, :], writes=['sgub'])
        wsT = TA.alloc([128, 4, 128], BF16)
        wsf = TA.alloc([128, 4, 128], F32)
        wsb = TA.alloc([128, 4, 128], BF16)
        for g in range(4):
            S.dma('sp', 'c4', wsf[:, g, :], sgu_w_in[l, g, :, :], writes=['wsf'])
        cp('dve', wsb.rearrange("p a b -> p (a b)"), wsf.rearrange("p a b -> p (a b)"), ['wsf'], ['wsb'])
        for g in range(4):
            tp(bkb(7)[:, g * 128:(g + 1) * 128], wsb[:, g, :], ['wsb'], ['b7'])
        cp('dve', wsT.rearrange("p a b -> p (a b)"), bkb(7)[:, 0:512], ['b7'], ['wsT'])
        S.dma('sp', 'c5', bcol, b_col[l, :, :], writes=['bcol'])
        act(expsink, expsink, AF.Exp, ['expsink'], ['expsink'])
        WA_ = Arena(arena_t, AR_ELEMS)
        WA_.seek(WC_OFF)
        wad = [WA_.alloc([128, 8, 512], F32) for _ in range(2)]
        wft = WA_.alloc([128, 2, 1024], BF16)
        for jc in range(6):
            wb = wad[jc % 2]
            wn = 'wad%d' % (jc % 2)
            S.dma('sp', wn, wb, w_ada[l, :, jc * 512:(jc + 1) * 512].rearrange("(k p) n -> p k n", p=128),
                  writes=[wn])
            for m in range(4):
                col = (jc * 4 + m) * 2
                for k in range(8):
                    mm(bk(0)[:, col:col + 2], wb[:, k, m * 128:(m + 1) * 128], scol[:, k, :],
                       k == 0, k == 7, [wn, 'scol'], ['b0'])
        tt('dve', modsb, bk(0)[:, 0:48].rearrange("p (a b) -> p a b", b=2),
           bcol.unsqueeze(2).to_broadcast([128, 24, 2]), ALU.add, ['b0', 'bcol'], ['modsb'])
        ts('dve', modsb[:, 8:16, :], modsb[:, 8:16, :], 1.0, None, ALU.add, None, ['modsb'], ['modsb'])
        ts('dve', modsb[:, 16:24, :], modsb[:, 16:24, :], 0.5, None, ALU.mult, None, ['modsb'], ['modsb'])
        for r in range(2):
            S.dma('sp', 'gs%d' % r, gate_s[r, :].rearrange("(k p) -> p k", p=128), modsb[:, 16:24, r],
                  reads=['modsb'], writes=['gate_s%d' % r], allow_slow_non_contiguous=True)
        if STOP == 'adaln':
            return True
        for k in range(8):
            S.dma('pool', 'wl%d' % (k % 4), wA[:, k, 512:2816], w_inA[l, k * 128:(k + 1) * 128, :],
                  writes=['wA'])
        for k in range(2):
            S.dma('pool', 'wl%d' % k, wft[:, k, :], w_faT[l, k * 128:(k + 1) * 128, :], writes=['wft'])
        for dk in range(8):
            b = 1 + dk % 2
            for k in range(2):
                mm(bk(b), wft[:, k, dk * 128:(dk + 1) * 128], csblk[:, k, :], k == 0, k == 1,
                   ['wft', 'csblk'], ['b%d' % b])
            cp('act' if dk % 2 else 'dve', wA[:, dk, 0:512], bk(b), ['b%d' % b], ['wA'])

        def twoA(shape, dt):
            return [TA.alloc(shape, dt) for _ in range(2)]
        xt = twoA([128, 1024], F32)
        junk = twoA([128, 1024], BF16)
        nb = twoA([128, 1024], BF16)
        hT = twoA([128, 8, 128], BF16)
        ss = twoA([128, 16], F32)
        faT_sb = twoA([128, 4, 128], BF16)
        qgf = twoA([128, 640], F32)
        sqf = twoA([128, 640], F32)
        t1f = twoA([128, 640], F32)
        t2f = twoA([128, 640], F32)
        qrb = twoA([128, 640], BF16)
        QT_sb = twoA([128, 512], BF16)
        KT_sb = twoA([128, 128], BF16)
        vn = twoA([128, 256], BF16)
        szb = twoA([128, 256], F32)
        uz = twoA([128, 256], F32)
        sza = twoA([128, 512], BF16)
        szc = twoA([128, 512], BF16)
        rC = twoA([128, 64], F32)
        rS = twoA([128, 64], F32)
        assert TA.off <= T_OFF + T_SIZE, ("passA transients", TA.off - T_OFF)

        def passA_chunks(t, isctx, s):
            r = 1 if isctx else 0
            ti = t - NT if isctx else t
            src = csrc if isctx else xsrc
            B = [4 * s + i for i in range(4)]
            bn = ['b%d' % b for b in B]
            sfx = '%d' % s
            R = lambda nm: nm + sfx
            xn, hn = R('xt'), R('hT')
            x_t, h_t, ss_ = xt[s], hT[s], ss[s]
            only_kv = isctx and last

            def c_loads():
                S.dma('sp', xn, x_t, src[ti * 128:(ti + 1) * 128, :], reads=['xres%d' % t], writes=[xn])

            def c_loads2():
                if not isctx:
                    S.dma('sp', R('rC'), rC[s], ropeC_in[ti * 128:(ti + 1) * 128, :], writes=[R('rC')])
                    S.dma('sp', R('rS'), rS[s], ropeS_in[ti * 128:(ti + 1) * 128, :], writes=[R('rS')])

            def c_rms():
                act(junk[s], x_t, AF.Square, [xn], [R('junk'), R('ss0')], accum=ss_[:, 0:1])
                ts('dve', ss_[:, 0:1], ss_[:, 0:1], 1.0 / D, EPS, ALU.mult, ALU.add, [R('ss0')], [R('ss0')])
                tt('pool', ss_[:, 0:1], ss_[:, 0:1], mhalf[:, 0:1], ALU.pow, [R('ss0'), 'mhalf'], [R('ss0')])
                act(nb[s], x_t, AF.Identity, [xn, R('ss0')], [R('nb')], scale=ss_[:, 0:1])

            def c_rms_b():
                tb = bkb(B[0])
                for k in range(8):
                    tp(tb[:, k * 128:(k + 1) * 128], nb[s][:, k * 128:(k + 1) * 128], [R('nb')], [bn[0]])
                for k in range(8):
                    if k % 2 == 0:
                        ts('dve', h_t[:, k, :], tb[:, k * 128:(k + 1) * 128], modsb[:, 8 + k, r:r + 1],
                           modsb[:, k, r:r + 1], ALU.mult, ALU.add, [bn[0], 'modsb'], [hn])
                    else:
                        act(h_t[:, k, :], tb[:, k * 128:(k + 1) * 128], AF.Identity, [bn[0], 'modsb'], [hn],
                            scale=modsb[:, 8 + k, r:r + 1], bias=modsb[:, k, r:r + 1])
                if not only_kv:
                    S.dma('act', R('hTs'), hT_s[t, :, :], h_t.rearrange("p k n -> p (k n)"), reads=[hn],
                          writes=['hT_s%d' % t])

            def c_fa():
                if not isctx:
                    for c4 in range(4):
                        for k in range(8):
                            mm(bk(B[1])[:, c4 * 128:(c4 + 1) * 128], wA[:, k, c4 * 128:(c4 + 1) * 128], h_t[:, k, :],
                               k == 0, k == 7, ['wA', hn], [bn[1]])
                    cp('act', faT_sb[s].rearrange("p a b -> p (a b)"), bk(B[1]), [bn[1]], [R('faT_sb')])
                    S.dma('act', R('faTs'), faT_s.rearrange("(c p) n -> p c n", p=128)[:, :, ti * 128:(ti + 1) * 128],
                          faT_sb[s], reads=[R('faT_sb')], writes=['faT_s%d' % ti])
                else:
                    for k in range(8):
                        mm(bk(B[1]), h_t[:, k, :], wA[:, k, 0:512], k == 0, k == 7, ['wA', hn], [bn[1]])
                    cp('act', fac[:, ti, :], bk(B[1]), [bn[1]], ['fac'])

            def proj(bi, c0, c1):
                for k in range(8):
                    mm(bk(B[bi])[:, 0:c1 - c0], h_t[:, k, :], wA[:, k, c0:c1], k == 0, k == 7, ['wA', hn], [bn[bi]])

            late = []

            def qk(ps, c0, nh, gain, gname, bname):
                w = nh * 64
                v3 = lambda ap: ap[:, c0:c0 + w].rearrange("p (h d) -> p h d", d=64)
                tt('dve', v3(qgf[s]), ps.rearrange("p (h d) -> p h d", d=64),
                   gain.unsqueeze(1).to_broadcast([128, nh, 64]), ALU.mult, [bname, gname], [R('qgf')])
                act(sqf[s][:, c0:c0 + w], ps, AF.Square, [bname], [R('sqf')])
                sname = R('ssq%d_' % c0)
                sv = ss_[:, 2:2 + nh] if c0 == 0 else ss_[:, 12:12 + nh]
                S.op('dve', lambda e: e.tensor_reduce(out=sv, in_=v3(sqf[s]), axis=AX.X, op=ALU.add),
                     reads=[R('sqf')], writes=[sname])
                ts('dve', sv, sv, 1.0 / 64, EPS, ALU.mult, ALU.add, [sname], [sname])
                tt('pool', sv, sv, mhalf[:, 0:nh], ALU.pow, [sname, 'mhalf'], [sname])
                if not isctx:
                    cb = rC[s].unsqueeze(1).to_broadcast([128, nh, 64])
                    tt('pool', v3(t1f[s]), v3(qgf[s]), cb, ALU.mult, [R('qgf'), R('rC')], [R('t1f')])
                    v5 = lambda ap: ap[:, c0:c0 + w].rearrange("p (h b i d) -> p h b i d", b=2, i=2, d=16)
                    s5 = rS[s].rearrange("p (b i d) -> p b i d", b=2, i=2)
                    for i in range(2):
                        tt('pool', v5(t2f[s])[:, :, :, i, :], v5(qgf[s])[:, :, :, 1 - i, :],
                           s5[:, :, i, :].unsqueeze(1).to_broadcast([128, nh, 2, 16]), ALU.mult,
                           [R('qgf'), R('rS')], [R('t2f')])
                    tt('pool', v3(t1f[s]), v3(t1f[s]), v3(t2f[s]), ALU.add, [R('t1f'), R('t2f')], [R('t1f')])
                    srcq, srcn = t1f[s], R('t1f')
                else:
                    srcq, srcn = qgf[s], R('qgf')
                late.append(lambda: tt('dve', v3(qrb[s]), v3(srcq), sv.unsqueeze(2).to_broadcast([128, nh, 64]),
                                       ALU.mult, [srcn, sname], [R('qrb')]))

            def c_qkv_proj():
                if not only_kv:
                    proj(2, 512, 1024)
                proj(3, 1024, 1536)

            def c_qpost():
                qk(bk(B[2]), 0, 8, gq, 'gq', bn[2])

            def c_kvpost():
                qk(bk(B[3])[:, 0:128], 512, 2, gk, 'gk', bn[3])
                vsrc = bk(B[3])[:, 128:256].rearrange("p (h d) -> p h d", d=64)
                if isctx:
                    cp('act', Vc[:, ti, :, 0:64], vsrc, [bn[3]], ['Vc'])
                else:
                    cp('act', Vst[s][:, :, 0:64], vsrc, [bn[3]], [R('Vst')])
                    S.dma('act', R('Vs'), V_s[ti * 128:(ti + 1) * 128, :], Vst[s].rearrange("p a b -> p (a b)"),
                          reads=[R('Vst')], writes=['V_s%d' % ti])
                if not only_kv:
                    act(junk[s][:, 0:256], bk(B[3])[:, 256:512], AF.Square, [bn[3]], [R('junk'), R('ss1')],
                        accum=ss_[:, 1:2])
                    ts('dve', ss_[:, 1:2], ss_[:, 1:2], 1.0 / 256, EPS, ALU.mult, ALU.add, [R('ss1')], [R('ss1')])
                    tt('pool', ss_[:, 1:2], ss_[:, 1:2], mhalf[:, 0:1], ALU.pow, [R('ss1'), 'mhalf'], [R('ss1')])

            def c_uz_proj():
                proj(0, 1536, 2048)
                proj(1, 2048, 2560)

            def c_uzpost():
                act(vn[s], bk(B[3])[:, 256:512], AF.Identity, [bn[3], R('ss1')], [R('vn')], scale=ss_[:, 1:2])
                act(szb[s], bk(B[0])[:, 256:512], AF.Silu, [bn[0]], [R('szb')])
                tt('dve', uz[s], bk(B[0])[:, 0:256], szb[s], ALU.mult, [bn[0], R('szb')], [R('uz')])
                act(szc[s], bk(B[1]), AF.Silu, [bn[1]], [R('szc')])
                S.dma('act', R('szcs'), szc_s[t * 128:(t + 1) * 128, :], szc[s], reads=[R('szc')],
                      writes=['szc_s%d' % t])

            def c_sgu():
                proj(2, 2560, 2816)
                for g in range(4):
                    mm(bk(B[3])[:, g * 128:(g + 1) * 128], wsT[:, g, :], vn[s][:, (g // 2) * 128:(g // 2) * 128 + 128],
                       True, True, ['wsT', R('vn')], [bn[3]])
                act(sza[s][:, 0:256], bk(B[2])[:, 0:256], AF.Silu, [bn[2]], [R('sza')])
                for g in range(4):
                    stt(sza[s][:, 256 + g * 64:256 + (g + 1) * 64],
                        bk(B[3])[:, g * 128 + (g % 2) * 64:g * 128 + (g % 2) * 64 + 64], sgub[:, g:g + 1],
                        uz[s][:, g * 64:(g + 1) * 64], ALU.add, ALU.mult, [bn[3], R('uz'), 'sgub'], [R('sza')])
                S.dma('act', R('szas'), sza_s[t * 128:(t + 1) * 128, :], sza[s], reads=[R('sza')],
                      writes=['sza_s%d' % t])

            def c_qkT():
                for f in late:
                    f()
                tb = bkb(B[0])
                if not only_kv:
                    for j in range(4):
                        tp(tb[:, j * 128:(j + 1) * 128], qrb[s][:, j * 128:(j + 1) * 128], [R('qrb')], [bn[0]])
                tp(tb[:, 512:640], qrb[s][:, 512:640], [R('qrb')], [bn[0]])
                if not only_kv:
                    cp('act', QT_sb[s], tb[:, 0:512], [bn[0]], [R('QT_sb')])
                    S.dma('act', R('QTs'), QT_s[t, :, :], QT_sb[s], reads=[R('QT_sb')], writes=['QT_s%d' % t])
                if isctx:
                    cp('act', KcT[:, ti * 128:(ti + 1) * 128], tb[:, 512:640], [bn[0]], ['KcT'])
                else:
                    cp('act', KT_sb[s], tb[:, 512:640], [bn[0]], [R('KT_sb')])
                    S.dma('act', R('KTs'), KT_s[:, ti * 128:(ti + 1) * 128], KT_sb[s], reads=[R('KT_sb')],
                          writes=['KT_s%d' % ti])

            def c_x1():
                for a in range(2):
                    S.dma('sp', 'x1l%d' % a, X1p[a * NT + ti:a * NT + ti + 1, :, :],
                          faT_s[a * 256:(a + 1) * 256, ti * 128:(ti + 1) * 128].unsqueeze(0),
                          reads=['faT_s%d' % ti], writes=['X1p', 'wad0', 'wad1', 'wft'])

            head = [(-EARLY_A, c_loads), (-EARLY_A + 0.5, c_rms), (-2.5, c_rms_b), (-2.4, c_loads2)]
            if only_kv:
                return head + [(1.0, c_qkv_proj), (2.0, c_kvpost), (8.0, c_qkT)]
            tlA = [(0.0, c_fa), (1.0, c_qkv_proj), (2.0, c_kvpost), (3.0, c_qpost), (4.0, c_uz_proj),
                   (5.0, c_uzpost), (6.0, c_sgu), (8.0, c_qkT)]
            if not isctx:
                tlA.append((9.0, c_x1))
            return head + tlA

        if STOP == 'dbg_ws':
            S.dma('sp', 'c0', ybT_s[NT, :, :], wsT.rearrange("p a b -> p (a b)")[:, 0:256], reads=['wsT'], writes=['dbgx'])
        if STOP == 'wA':
            return True
        seqA = [(t, True) for t in range(NT, NTT)] + [(t, False) for t in range(NT)]
        EARLY_A = 5.0
        listsA = [passA_chunks(t, ic, i % 2) for i, (t, ic) in enumerate(seqA)]
        HA = 6
        assert max(c[-1][0] for c in listsA) + 1 <= 2 * HA and EARLY_A < HA
        itemsA = sorted((i * HA + c[j][0], i, j) for i, c in enumerate(listsA) for j in range(len(c)))
        for _, i, j in itemsA:
            listsA[i][j][1]()
        if STOP == 'passA':
            return True

        S.barrier()
        FA = Arena(arena_t, AR_ELEMS)
        FA.seek(WA_OFF)
        Gc = [FA.alloc([128, K2, 64], BF16) for _ in range(2)]
        Yc = [FA.alloc([128, NT, 64], BF16) for _ in range(2)]
        KG = min(8, NT)
        Mst = [FA.alloc([128, KG, 2, 128], BF16) for _ in range(2)]
        Yct = FA.alloc([128, 256], BF16)
        assert FA.off <= WC_OFF
        ysc = 1.0 / np.sqrt(float(L))
        if not last:
            for kt in range(2):
                i = 0
                for nt in range(2):
                    for a in range(2):
                        mm(bk(7)[:, 0:256], c256[:, nt, a, kt * 128:(kt + 1) * 128], fac[:, nt, a * 256:(a + 1) * 256],
                           i == 0, i == 3, ['c256', 'fac'], ['b7'])
                        i += 1
                act(Yct, bk(7)[:, 0:256], AF.Copy, ['b7'], ['Yct'], scale=1.0 / 16.0)
                S.dma('sp', 'Ycts', Y_s[:, L + kt * 128:L + (kt + 1) * 128, :].rearrange("c n e -> n c e"),
                      Yct.rearrange("p (c e) -> p c e", c=4), reads=['Yct'], writes=['Y_sc%d' % kt])
        def load_passC_weights():
            dead = ['X1p', 'wad0', 'wad1', 'wft']
            for k in range(8):
                S.dma('pool', 'wl%d' % (k % 4), wg[:, k, :], w_g[l, k * 128:(k + 1) * 128, :], writes=['wg'] + dead)
            for k in range(2):
                S.dma('pool', 'wl%d' % k, wpa[:, k, :], w_pa[l, k * 128:(k + 1) * 128, :], writes=['wpa'] + dead)
                S.dma('pool', 'wl%d' % (2 + k), wpb[:, k, :], w_pb[l, k * 128:(k + 1) * 128, :], writes=['wpb'] + dead)
            for k in range(4):
                S.dma('pool', 'wl%d' % k, wpc[:, k, :], w_pc[l, k * 128:(k + 1) * 128, :], writes=['wpc'] + dead)
            for k in range(8):
                S.dma('pool', 'wl%d' % (k % 4), wout[:, k, :], w_out[l, k * 128:(k + 1) * 128, :],
                      writes=['wout'] + dead)

        mcnt = 0
        for c in range(4):
            g_t = Gc[c % 2]
            gn = 'Gc%d' % (c % 2)
            y_t = Yc[c % 2]
            yn = 'Yc%d' % (c % 2)
            for q4 in range(16):
                b = q4 % 2
                for i in range(4):
                    ch = c * 64 + q4 * 4 + i
                    mm(bk(b)[:, i * K2:(i + 1) * K2], X1p[0:K2, ch, :], r1[0:K2, :], True, True,
                       ['X1p', 'r1'], ['b%d' % b])
                cp('act' if q4 % 2 else 'dve',
                   g_t[:, :, q4 * 4:q4 * 4 + 4].rearrange("p j c -> p c j"),
                   bk(b)[:, 0:4 * K2].rearrange("p (c j) -> p c j", c=4), ['b%d' % b], [gn])
            if c == 3:
                load_passC_weights()
            for kg in range(NT // KG):
                m_t = Mst[mcnt % 2]
                mn = 'Mst%d' % (mcnt % 2)
                mcnt += 1
                S.dma('sp', mn, m_t, m2_in[:, kg * KG:(kg + 1) * KG, :, :], writes=[mn])
                b = 2 + kg % 2
                for kk in range(KG):
                    k2 = kg * KG + kk
                    for r_ in range(2):
                        mm(bk(b)[:, kk * 64:(kk + 1) * 64], m_t[:, kk, r_, :], g_t[:, 2 * k2 + r_, :],
                           r_ == 0, r_ == 1, [mn, gn], ['b%d' % b])
                act(y_t[:, kg * KG:(kg + 1) * KG, :].rearrange("p a b -> p (a b)"), bk(b)[:, 0:KG * 64], AF.Copy,
                    ['b%d' % b], [yn], scale=float(ysc))
            S.dma('sp', yn + 's', Y_s[c, 0:L, :].rearrange("(k1 k2) e -> k1 k2 e", k2=NT), y_t,
                  reads=[yn], writes=['Y_s'])

        if STOP == 'fourier':
            return True
        S.barrier(skip=('wl0', 'wl1', 'wl2', 'wl3'))
        CA = Arena(arena_t, AR_ELEMS)
        CA.seek(WA_OFF)
        gate_bc = [CA.alloc([128, 1024], F32) for _ in range(2)]
        for r in range(2):
            S.dma('sp', 'gb%d' % r, gate_bc[r], gate_s[r, :].partition_broadcast(128), reads=['gate_s%d' % r],
                  writes=['gate_bc%d' % r])
        def two(shape, dt):
            return [CA.alloc(shape, dt) for _ in range(2)]
        xt = two([128, 1024], F32)
        hT = two([128, 8, 128], BF16)
        QTt = two([128, 512], BF16)
        KTw = two([128, 3, 128], BF16)
        Vw = two([128, 3, 130], BF16)
        szc = two([128, 512], BF16)
        sza = two([128, 512], BF16)
        Yt = two([128, 256], BF16)
        E = two([128, 5, 2, 512], BF16)
        tg = two([128, 3072], BF16)
        den = two([128, 8], F32)
        atf = two([128, 512], F32)
        ycb = two([128, 512], BF16)
        yab = two([128, 256], BF16)
        yT = two([128, 8, 128], BF16)
        m1 = two([128, 512], F32)
        m2 = two([128, 512], F32)
        m3 = two([128, 512], F32)
        mg = two([128, 1024], BF16)
        mT = two([128, 8, 128], BF16)
        xo = two([128, 1024], F32)
        assert CA.off <= WC_OFF, ("passC transients", CA.off, WC_OFF)

        def passC_chunks(t, isctx, s):
            r = 1 if isctx else 0
            ti = t - NT if isctx else t
            src = csrc if isctx else xsrc
            dst = xc_s if isctx else y
            B = [4 * s + i for i in range(4)]
            bn = ['b%d' % b for b in B]
            sfx = '%d' % s
            xn, hn = 'xt' + sfx, 'hT' + sfx
            R = lambda nm: nm + sfx
            kbs = []
            if not isctx:
                lo = max(ti - 1, 0)
                hi = min(ti + 1, NT - 1)
                nb_ = hi - lo + 1
                for u in range(lo, hi + 1):
                    mk = None if u == ti else (maskP if u < ti else maskN)
                    kbs.append((KTw[s][:, u - lo, :], Vw[s][:, u - lo, :].rearrange("p (h e) -> p h e", h=2), mk,
                                [R('KTw')], [R('Vw')]))
            for u in range(CT):
                kbs.append((KcT[:, u * 128:(u + 1) * 128], Vc[:, u, :, :], None, ['KcT'], ['Vc']))
            nk = len(kbs)

            def c_loads_early():
                S.dma('sp', hn, hT[s].rearrange("p k n -> p (k n)"), hT_s[t, :, :], reads=['hT_s%d' % t], writes=[hn])
                S.dma('sp', R('QT'), QTt[s], QT_s[t, :, :], reads=['QT_s%d' % t], writes=[R('QT')])
                if not isctx:
                    S.dma('sp', R('KTw'), KTw[s][:, 0:nb_, :].rearrange("p a b -> p (a b)"),
                          KT_s[:, lo * 128:(hi + 1) * 128], reads=['KT_s%d' % u for u in range(lo, hi + 1)],
                          writes=[R('KTw')])

            def c_loads():
                S.dma('sp', xn, xt[s], src[ti * 128:(ti + 1) * 128, :], reads=['xres%d' % t], writes=[xn])
                S.dma('sp', R('szc'), szc[s], szc_s[t * 128:(t + 1) * 128, :], reads=['szc_s%d' % t],
                      writes=[R('szc')])
                S.dma('sp', R('sza'), sza[s], sza_s[t * 128:(t + 1) * 128, :], reads=['sza_s%d' % t],
                      writes=[R('sza')])
                S.dma('sp', R('Yt'), Yt[s].rearrange("p (c e) -> p c e", c=4),
                      Y_s[:, t * 128:(t + 1) * 128, :].rearrange("c n e -> n c e"),
                      reads=['Y_s', 'Y_sc0', 'Y_sc1'], writes=[R('Yt')])
                if not isctx:
                    S.dma('sp', R('Vw'), Vw[s][:, 0:nb_, :],
                          V_s[lo * 128:(hi + 1) * 128, :].rearrange("(a p) e -> p a e", p=128),
                          reads=['V_s%d' % u for u in range(lo, hi + 1)], writes=[R('Vw')])

            def c_score(bi):
                kT, vv, mk, kr, vr = kbs[bi]
                for kv in range(2):
                    b = B[kv]
                    mm(bk(b), kT[kv * 64:(kv + 1) * 64, :], QTt[s][kv * 64:(kv + 1) * 64, :], True, mk is None,
                       kr + [R('QT')], [bn[kv]])
                    if mk is not None:
                        mm(bk(b), ident, mk, False, True, ['ident', 'maskP', 'maskN'], [bn[kv]])
                for kv in range(2):
                    act(E[s][:, bi, kv, :], bk(B[kv]), AF.Exp, [bn[kv]], [R('E')], scale=0.125)

            def c_gate(c):
                b = B[2 + c % 2]
                for k in range(8):
                    mm(bk(b), hT[s][:, k, :], wg[:, k, c * 512:(c + 1) * 512], k == 0, k == 7, ['wg', hn],
                       [bn[2 + c % 2]])
                act(tg[s][:, c * 512:(c + 1) * 512], bk(b), AF.Tanh, [bn[2 + c % 2]], [R('tg')], scale=0.5)

            def c_pv(kv):
                ob = bk(B[2 + kv])[:, 0:260].rearrange("p (j e) -> p j e", e=65)
                for j in range(4):
                    for bi, (kT, vv, mk, kr, vr) in enumerate(kbs):
                        mm(ob[:, j, :], E[s][:, bi, kv, j * 128:(j + 1) * 128], vv[:, kv, :], bi == 0,
                           bi == nk - 1, [R('E')] + vr, [bn[2 + kv]])
                tt('dve', den[s][:, kv * 4:(kv + 1) * 4], ob[:, :, 64], expsink[:, kv * 4:(kv + 1) * 4], ALU.add,
                   [bn[2 + kv], 'expsink'], [R('den')])

            def c_norm():
                S.op('dve', lambda e: e.reciprocal(out=den[s], in_=den[s]), reads=[R('den')], writes=[R('den')])
                for kv in range(2):
                    ob = bk(B[2 + kv])[:, 0:260].rearrange("p (j e) -> p j e", e=65)
                    tt('dve', atf[s][:, kv * 256:(kv + 1) * 256].rearrange("p (j d) -> p j d", d=64), ob[:, :, 0:64],
                       den[s][:, kv * 4:(kv + 1) * 4].unsqueeze(2).to_broadcast([128, 4, 64]), ALU.mult,
                       [bn[2 + kv], R('den')], [R('atf')])
                tt('pool', ycb[s], atf[s], szc[s], ALU.mult, [R('atf'), R('szc')], [R('ycb')])
                tt('pool', yab[s], Yt[s], sza[s][:, 0:256], ALU.mult, [R('Yt'), R('sza')], [R('yab')])

            def c_tr():
                tb = bkb(B[0])
                for j in range(2):
                    tp(tb[:, j * 128:(j + 1) * 128], yab[s][:, j * 128:(j + 1) * 128], [R('yab')], [bn[0]])
                for j in range(4):
                    tp(tb[:, (2 + j) * 128:(3 + j) * 128], ycb[s][:, j * 128:(j + 1) * 128], [R('ycb')], [bn[0]])
                for j in range(2):
                    tp(tb[:, (6 + j) * 128:(7 + j) * 128], sza[s][:, 256 + j * 128:256 + (j + 1) * 128],
                       [R('sza')], [bn[0]])
                cp('act', yT[s].rearrange("p a b -> p (a b)"), tb, [bn[0]], [R('yT')])

            def c_proj(hD):
                cs = slice(hD * 512, (hD + 1) * 512)
                for k in range(2):
                    mm(bk(B[1]), yT[s][:, k, :], wpa[:, k, cs], k == 0, k == 1, [R('yT'), 'wpa'], [bn[1]])
                for k in range(2):
                    mm(bk(B[2]), yT[s][:, 6 + k, :], wpb[:, k, cs], k == 0, k == 1, [R('yT'), 'wpb'], [bn[2]])
                for k in range(4):
                    mm(bk(B[3]), yT[s][:, 2 + k, :], wpc[:, k, cs], k == 0, k == 3, [R('yT'), 'wpc'], [bn[3]])
                stt(m1[s], tg[s][:, hD * 512:(hD + 1) * 512], 1.0, bk(B[1]), ALU.add, ALU.mult, [R('tg'), bn[1]],
                    [R('m1')])
                stt(m2[s], tg[s][:, 1024 + hD * 512:1024 + (hD + 1) * 512], 1.0, bk(B[2]), ALU.add, ALU.mult,
                    [R('tg'), bn[2]], [R('m2')])
                stt(m3[s], tg[s][:, 2048 + hD * 512:2048 + (hD + 1) * 512], 1.0, bk(B[3]), ALU.add, ALU.mult,
                    [R('tg'), bn[3]], [R('m3')])
                tt('pool', m1[s], m1[s], m2[s], ALU.add, [R('m1'), R('m2')], [R('m1')])
                tt('pool', mg[s][:, cs], m1[s], m3[s], ALU.add, [R('m1'), R('m3')], [R('mg')])

            def c_mt():
                tb = bkb(B[0])
                for k in range(8):
                    tp(tb[:, k * 128:(k + 1) * 128], mg[s][:, k * 128:(k + 1) * 128], [R('mg')], [bn[0]])
                cp('act', mT[s].rearrange("p a b -> p (a b)"), tb, [bn[0]], [R('mT')])

            def c_out(hD):
                cs = slice(hD * 512, (hD + 1) * 512)
                b = B[1 + hD]
                for k in range(8):
                    mm(bk(b), mT[s][:, k, :], wout[:, k, cs], k == 0, k == 7, [R('mT'), 'wout'], [bn[1 + hD]])
                tt('dve', xo[s][:, cs], bk(b), gate_bc[r][:, cs], ALU.mult, [bn[1 + hD], 'gate_bc%d' % r],
                   [R('xo')])

            def c_fin():
                tt('pool', xo[s], xo[s], xt[s], ALU.add, [R('xo'), xn], [R('xo')])
                S.dma('pool', R('xo'), dst[ti * 128:(ti + 1) * 128, :], xo[s], reads=[R('xo')],
                      writes=['xres%d' % t])

            ch = [c_loads]
            sc = [lambda bi=bi: c_score(bi) for bi in range(nk)]
            gg = [lambda c=c: c_gate(c) for c in range(6)]
            while sc or gg:
                if sc:
                    ch.append(sc.pop(0))
                if gg:
                    ch.append(gg.pop(0))
            n_front = len(ch)
            tl = [(float(j), f) for j, f in enumerate(ch)]
            j = float(n_front)
            for f, slack in [(lambda: c_pv(0), 0), (lambda: c_pv(1), 0), (c_norm, 0), (c_tr, 2),
                             (lambda: c_proj(0), 0), (lambda: c_proj(1), 0), (c_mt, 2),
                             (lambda: c_out(0), 0), (lambda: c_out(1), 0), (c_fin, 0)]:
                j += slack
                tl.append((j, f))
                j += 1
            tl.insert(0, (MAXFRONT_C - 2 * HC + 0.5, c_loads_early))
            return tl

        seqC = ([(t, True) for t in range(NT, NTT)] if not last else []) + [(t, False) for t in range(NT)]
        MAXFRONT_C = 11.0
        HC = 13
        lists = [passC_chunks(t, ic, i % 2) for i, (t, ic) in enumerate(seqC)]
        assert max(c[-1][0] for c in lists) + 1 <= 2 * HC
        items = sorted((i * HC + c[j][0], i, j) for i, c in enumerate(lists) for j in range(len(c)))
        for _, i, j in items:
            lists[i][j][1]()

    for l in range(NL):
        if layer(l):
            break
    S.finish('sp')
    es.close()
    return nc


def _bf(a):
    return np.ascontiguousarray(a.astype(ml_dtypes.bfloat16))


def _tables(NT):
    L = NT * 128
    K2 = 2 * NT
    t = {}
    t['ident'] = _bf(np.eye(128, dtype=np.float32))
    j = np.arange(64)
    ang = 2 * np.pi * np.outer(j, j) / 64.0
    cs = np.zeros((256, 512), np.float64)
    for g in range(4):
        cs[g * 64:(g + 1) * 64, g * 64:(g + 1) * 64] = np.cos(ang) / 8.0
        cs[g * 64:(g + 1) * 64, 256 + g * 64:256 + (g + 1) * 64] = np.sin(ang) / 8.0
    t['csblk'] = _bf(cs)
    n2 = np.arange(NT)
    th = 2 * np.pi * np.outer(n2, n2) / NT
    r1 = np.zeros((2, NT, NT, 2), np.float64)
    r1[0, :, :, 0] = np.cos(th)
    r1[1, :, :, 0] = -np.sin(th)
    r1[0, :, :, 1] = np.sin(th)
    r1[1, :, :, 1] = np.cos(th)
    t['r1'] = _bf(r1.reshape(K2, K2))
    n1 = np.arange(128)[:, None, None]
    k2 = np.arange(NT)[None, :, None]
    k1 = np.arange(128)[None, None, :]
    kk = (NT * k1 + k2).astype(np.int64)
    ph = 2 * np.pi * ((n1 * kk) % L) / L
    m2 = np.stack([np.cos(ph), -np.sin(ph)], axis=2)
    t['m2'] = _bf(m2)
    n = (np.arange(2)[None, :, None] * 128 + np.arange(128)[:, None, None])
    k = np.arange(256)[None, None, :]
    ph = 2 * np.pi * ((n * k) % 256) / 256.0
    t['c256'] = _bf(np.stack([np.cos(ph), -np.sin(ph)], axis=2))
    pos = np.arange(L)
    row = (pos // 64).astype(np.float32)
    col = (pos % 64).astype(np.float32)
    inv = (10000.0 ** (-np.arange(16, dtype=np.float32) / 16)).astype(np.float32)
    ar = row[:, None] * inv[None, :]
    ac = col[:, None] * inv[None, :]
    t['ropeC'] = np.concatenate([np.cos(ar), np.cos(ar), np.cos(ac), np.cos(ac)], axis=1).astype(np.float32)
    t['ropeS'] = np.concatenate([-np.sin(ar), np.sin(ar), -np.sin(ac), np.sin(ac)], axis=1).astype(np.float32)
    p = np.arange(128)[:, None]
    f = (np.arange(512) % 128)[None, :]
    t['maskP'] = _bf(np.where(p >= f, 0.0, NEG))
    t['maskN'] = _bf(np.where(p <= f, 0.0, NEG))
    return t


def _prep_shared(NL, w_ada, b_ada, w_in, sgu_w, sgu_b, q_norm_g, k_norm_g, attn_sink, w_pa, w_pb, w_pc, w_out):
    f = lambda a: np.ascontiguousarray(np.asarray(a, dtype=np.float32))
    w_in = np.asarray(w_in, dtype=np.float32)
    qperm = []
    for j in range(4):
        for kv in range(2):
            h = kv * 4 + j
            qperm.extend(range(1280 + h * 64, 1280 + (h + 1) * 64))
    colsA = (qperm + list(range(1792, 1920)) + list(range(1920, 2048)) + list(range(768, 1024))
             + list(range(512, 768)) + list(range(1024, 1280)) + list(range(2048, 2560)) + list(range(256, 512)))
    sh = {}
    sh['w_ada'] = f(w_ada[:NL])
    b = np.asarray(b_ada, dtype=np.float32)[:NL]
    sh['b_col'] = f(b.reshape(NL, 24, 128).transpose(0, 2, 1))
    sh['w_inA'] = f(w_in[:NL][:, :, colsA])
    sh['w_faT'] = f(w_in[:NL][:, :, 0:256].transpose(0, 2, 1))
    sh['w_g'] = f(w_in[:NL][:, :, 2560:5632])
    sh['w_pa'] = f(w_pa[:NL])
    sh['w_pb'] = f(w_pb[:NL])
    sh['w_pc'] = f(w_pc[:NL])
    sh['w_out'] = f(w_out[:NL])
    sh['sgu_w'] = f(sgu_w[:NL])
    sh['sgu_bT'] = f(np.asarray(sgu_b, dtype=np.float32)[:NL].transpose(0, 2, 1))
    sh['qg'] = f(q_norm_g[:NL])
    sh['kg'] = f(k_norm_g[:NL])
    sh['sink'] = f(attn_sink[:NL])
    return sh


_CACHE = {}


def run(x, c, ctx, c_ctx, NL, **params):
    x = np.asarray(x, dtype=np.float32)
    B, L, _ = x.shape
    NT = L // 128
    key = (NT, NL)
    if key not in _CACHE:
        _CACHE[key] = build(NT, NL)
    nc = _CACHE[key]
    shared = _prep_shared(NL, **params)
    shared.update(_tables(NT))
    c = np.asarray(c, dtype=np.float32)
    c_ctx = np.asarray(c_ctx, dtype=np.float32)
    ctx = np.asarray(ctx, dtype=np.float32)
    in_maps = []
    for core in range(8):
        b = core % B
        m = dict(shared)
        m['x'] = np.ascontiguousarray(x[b])
        m['ctx'] = np.ascontiguousarray(ctx[b])
        cc = np.stack([c[b].reshape(8, 128).T, c_ctx.reshape(8, 128).T], axis=2)
        m['ccol'] = np.ascontiguousarray(cc.astype(np.float32))
        in_maps.append(m)
    res = run_bass_kernel_spmd(nc, in_maps, core_ids=list(range(8)))
    if DEBUG:
        return res.results
    return np.stack([res.results[b]["y"] for b in range(B)], axis=0).astype(np.float32)


def kernel(x, c, ctx, c_ctx, w_ada, b_ada, w_in, sgu_w, sgu_b, q_norm_g, k_norm_g,
           attn_sink, w_pa, w_pb, w_pc, w_out):
    return run(x, c, ctx, c_ctx, 4, w_ada=w_ada, b_ada=b_ada, w_in=w_in, sgu_w=sgu_w, sgu_b=sgu_b,
               q_norm_g=q_norm_g, k_norm_g=k_norm_g, attn_sink=attn_sink, w_pa=w_pa, w_pb=w_pb,
               w_pc=w_pc, w_out=w_out)
```
